# Optimizing a Trainium2 kernel written in Bass

```python
import math
import jax
import jax.numpy as jnp
from jax import lax
import numpy as np

D_MODEL = 1024
BATCH = 2
SEQ = 8192
DEPTH = 4
DEC_BATCH = 32
DEC_SEQ = 8
PAST_LEN = 8192
PAGE_SIZE = 128

N_A_LAYERS = DEPTH // 2
N_B_LAYERS = DEPTH - N_A_LAYERS

M_EXPAND = 2
M_D_INNER = M_EXPAND * D_MODEL
M_HEADDIM = 64
M_HEADS = M_D_INNER // M_HEADDIM
M_NGROUPS = 4
M_D_STATE = 128
M_CONV = 4
M_GN = M_NGROUPS * M_D_STATE
M_CONV_DIM = M_D_INNER + 2 * M_GN
M_IN_DIM = 2 * M_D_INNER + 2 * M_GN + M_HEADS
SSD_CHUNK = 128

HEAD_DIM = 64
HEADS_PER_GROUP = D_MODEL // HEAD_DIM
DIL_PAIRS = ((128, 1), (512, 4), (2048, 16))
N_DIL = len(DIL_PAIRS)
ATT_HEADS = N_DIL * HEADS_PER_GROUP
ATT_WIDTH = ATT_HEADS * HEAD_DIM
ATT_OUT = HEADS_PER_GROUP * HEAD_DIM
WINDOW_MAX = 2048
ATT_BLOCK = 128
ROPE_THETA = 10000.0

D_FF = 2816
FFN_CONV = 3

EPS = 1e-6

kernel_name = 'yoco_ssd_dilated_convffn_step'


def rmsnorm(x, g):
    xf = x.astype(jnp.float32)
    y = xf * lax.rsqrt(jnp.mean(xf * xf, axis=-1, keepdims=True) + EPS)
    return (y * g.astype(jnp.float32)).astype(x.dtype)


def modulate(x, shift, scale):
    return x * (1 + scale[:, None]) + shift[:, None]


def causal_dwconv(x, prev, w, b):
    K = w.shape[0]
    T = x.shape[1]
    xc = jnp.concatenate([prev.astype(x.dtype), x], axis=1)
    y = sum(xc[:, k:k + T] * w[k] for k in range(K)) + b
    return y, xc[:, T:]


def rope(x, pos):
    dh = x.shape[-1]
    half = dh // 2
    inv = ROPE_THETA ** (-jnp.arange(half, dtype=jnp.float32) * (2.0 / dh))
    ang = pos.astype(jnp.float32)[:, None] * inv[None, :]
    cos = jnp.cos(ang)[None, :, None, :]
    sin = jnp.sin(ang)[None, :, None, :]
    xf = x.astype(jnp.float32)
    x1, x2 = xf[..., :half], xf[..., half:]
    return jnp.concatenate([x1 * cos - x2 * sin, x2 * cos + x1 * sin], axis=-1).astype(x.dtype)


def ssd_scan(x, dt, A, Bm, Cm, h0):
    f32 = jnp.float32
    b, L, H, P = x.shape
    G, N = Bm.shape[2], Bm.shape[3]
    E = H // G
    Q = min(SSD_CHUNK, L)
    nc = -(-L // Q)
    pad = nc * Q - L

    def padt(t):
        return jnp.pad(t.astype(f32), [(0, 0), (0, pad)] + [(0, 0)] * (t.ndim - 2))

    xc = padt(x).reshape(b, nc, Q, G, E, P)
    dtc = padt(dt).reshape(b, nc, Q, G, E)
    Bc = padt(Bm).reshape(b, nc, Q, G, N)
    Cc = padt(Cm).reshape(b, nc, Q, G, N)
    acum = jnp.cumsum(dtc * A.astype(f32).reshape(G, E), axis=2)
    seg = acum[:, :, :, None] - acum[:, :, None, :]
    causal = jnp.tril(jnp.ones((Q, Q), dtype=bool))[:, :, None, None]
    Lm = jnp.exp(jnp.where(causal, seg, -jnp.inf))
    xdt = xc * dtc[..., None]
    CB = jnp.einsum('bclgn,bcsgn->bclsg', Cc, Bc)
    y_diag = jnp.einsum('bclsge,bcsgep->bclgep', CB[..., None] * Lm, xdt)
    xdt_dec = xdt * jnp.exp(acum[:, :, -1:] - acum)[..., None]
    states = jnp.einsum('bclgn,bclgep->bcgepn', Bc, xdt_dec)
    chunk_decay = jnp.exp(acum[:, :, -1])

    def step(h, inp):
        dec, s = inp
        return h * dec[..., None, None] + s, h

    h_last, h_prev = lax.scan(step, h0.astype(f32).reshape(b, G, E, P, N),
                              (jnp.moveaxis(chunk_decay, 1, 0), jnp.moveaxis(states, 1, 0)))
    h_prev = jnp.moveaxis(h_prev, 0, 1)
    y_off = jnp.einsum('bclgn,bcgepn->bclgep', Cc, h_prev) * jnp.exp(acum)[..., None]
    y = (y_diag + y_off).reshape(b, nc * Q, H, P)[:, :L]
    return y, h_last.reshape(b, H, P, N)


def mamba2_mixer(xn, ssm0, conv0, w_in, conv_w, conv_b, dt_bias, A_log, Dskip, norm_w, w_out):
    f32 = jnp.float32
    B_, T, _ = xn.shape
    zxbcdt = xn @ w_in
    z = zxbcdt[..., :M_D_INNER]
    xBC = zxbcdt[..., M_D_INNER:M_D_INNER + M_CONV_DIM]
    dt = zxbcdt[..., M_D_INNER + M_CONV_DIM:]
    xBC, conv_new = causal_dwconv(xBC, conv0, conv_w, conv_b)
    xBC = jax.nn.silu(xBC)
    xs = xBC[..., :M_D_INNER].reshape(B_, T, M_HEADS, M_HEADDIM)
    Bm = xBC[..., M_D_INNER:M_D_INNER + M_GN].reshape(B_, T, M_NGROUPS, M_D_STATE)
    Cm = xBC[..., M_D_INNER + M_GN:].reshape(B_, T, M_NGROUPS, M_D_STATE)
    dt = jax.nn.softplus(dt.astype(f32) + dt_bias.astype(f32))
    A = -jnp.exp(A_log.astype(f32))
    y, ssm_new = ssd_scan(xs, dt, A, Bm, Cm, ssm0)
    y = y + Dskip.astype(f32)[:, None] * xs.astype(f32)
    yg = (y.reshape(B_, T, M_D_INNER) * jax.nn.silu(z.astype(f32))).reshape(
        B_, T, M_NGROUPS, M_D_INNER // M_NGROUPS)
    yg = yg * lax.rsqrt(jnp.mean(yg * yg, axis=-1, keepdims=True) + EPS)
    y = (yg.reshape(B_, T, M_D_INNER) * norm_w.astype(f32)).astype(xn.dtype)
    return y @ w_out, ssm_new.astype(ssm0.dtype), conv_new


def conv_ffn(xn, prev, w_up, conv_w, conv_b, w_down):
    u, new_prev = causal_dwconv(xn @ w_up, prev, conv_w, conv_b)
    return (jax.nn.silu(u[..., :D_FF]) * u[..., D_FF:]) @ w_down, new_prev


def dilated_band(q, k, v, window, dil):
    f32 = jnp.float32
    B_, S, H, Dh = q.shape
    band = window // dil
    span = dil * ATT_BLOCK
    L = -(-S // span) * span
    n = L // dil
    nb = n // ATT_BLOCK

    def split(t):
        t = jnp.pad(t, [(0, 0), (0, L - S), (0, 0), (0, 0)])
        return t.reshape(B_, n, dil, H, Dh).transpose(0, 2, 1, 3, 4).reshape(B_, dil, nb, ATT_BLOCK, H, Dh)

    def band_keys(t):
        tp = jnp.pad(t, [(0, 0), (0, 0), (1, 0), (0, 0), (0, 0), (0, 0)])
        return jnp.concatenate([tp[:, :, :-1], tp[:, :, 1:]], axis=3)

    qb = split(q)
    kk = band_keys(split(k))
    vv = band_keys(split(v))
    s = jnp.einsum('brcqhe,brckhe->brchqk', qb, kk).astype(f32) * (Dh ** -0.5)
    i = jnp.arange(ATT_BLOCK)[:, None]
    j = jnp.arange(2 * ATT_BLOCK)[None, :]
    dist = ATT_BLOCK + i - j
    c = jnp.arange(nb)[:, None, None]
    valid = (dist >= 0) & (dist <= band) & ((c > 0) | (j >= ATT_BLOCK))
    s = jnp.where(valid[None, None, :, None], s, -jnp.inf)
    m = s.max(axis=-1)
    p = jnp.exp(s - m[..., None])
    l = p.sum(axis=-1)
    o = jnp.einsum('brchqk,brckhe->brcqhe', p, vv.astype(f32))

    def merge(t):
        rest = t.shape[4:]
        t = t.reshape((B_, dil, n) + rest).swapaxes(1, 2).reshape((B_, L) + rest)
        return t[:, :S]

    return merge(m.swapaxes(-1, -2)), merge(l.swapaxes(-1, -2)), merge(o)


def dilated_gather(q, k_all, v_all, window, dil, q_off):
    f32 = jnp.float32
    B_, T, H, Dh = q.shape
    band = window // dil
    idx = q_off + jnp.arange(T)[:, None] - dil * jnp.arange(band + 1)[None, :]
    valid = idx >= 0
    idx = jnp.maximum(idx, 0)
    kg = jnp.take(k_all, idx, axis=1)
    vg = jnp.take(v_all, idx, axis=1)
    s = jnp.einsum('bthe,btkhe->bthk', q, kg).astype(f32) * (Dh ** -0.5)
    s = jnp.where(valid[None, :, None, :], s, -jnp.inf)
    m = s.max(axis=-1)
    p = jnp.exp(s - m[..., None])
    l = p.sum(axis=-1)
    o = jnp.einsum('bthk,btkhe->bthe', p, vg.astype(f32))
    return m, l, o


def combine_groups(parts):
    m = jnp.stack([pt[0] for pt in parts])
    l = jnp.stack([pt[1] for pt in parts])
    o = jnp.stack([pt[2] for pt in parts])
    w = jnp.exp(m - m.max(axis=0))
    return jnp.einsum('gbth,gbthe->bthe', w, o) / jnp.sum(w * l, axis=0)[..., None]


def dilated_attention(xn, pos, kv, w_q, w_o):
    B_, T, _ = xn.shape
    k_src, v_src, off = kv
    q = rope((xn @ w_q).reshape(B_, T, ATT_HEADS, HEAD_DIM), pos)
    parts = []
    for g, (win, dil) in enumerate(DIL_PAIRS):
        hs = slice(g * HEADS_PER_GROUP, (g + 1) * HEADS_PER_GROUP)
        if off is None:
            parts.append(dilated_band(q[:, :, hs], k_src[:, :, hs], v_src[:, :, hs], win, dil))
        else:
            parts.append(dilated_gather(q[:, :, hs], k_src[:, :, hs], v_src[:, :, hs], win, dil, off))
    o = combine_groups(parts)
    return o.reshape(B_, T, ATT_OUT).astype(xn.dtype) @ w_o


def trunk(x, c, pos, ssm0, conv0, ffn0, kv_past, p):
    B_, T, _ = x.shape
    h = x
    cs = jax.nn.silu(c)
    ssm_out, conv_out, ffn_out = [], [], []
    kv = None
    k_new = None
    v_new = None
    for i in range(DEPTH):
        mod = cs @ p['ada_w'][i] + p['ada_b'][i]
        sh1, sc1, g1, sh2, sc2, g2 = jnp.split(mod, 6, axis=-1)
        xn = modulate(rmsnorm(h, p['norm_mix'][i]), sh1, sc1)
        if i < N_A_LAYERS:
            y, s_new, cv_new = mamba2_mixer(xn, ssm0[i], conv0[i], p['m_w_in'][i], p['m_conv_w'][i],
                                           p['m_conv_b'][i], p['m_dt_bias'][i], p['m_A_log'][i],
                                           p['m_D'][i], p['m_norm'][i], p['m_w_out'][i])
            ssm_out.append(s_new)
            conv_out.append(cv_new)
        else:
            jb = i - N_A_LAYERS
            y = dilated_attention(xn, pos, kv, p['w_q'][jb], p['w_o'][jb])
        h = h + g1[:, None] * y
        xn = modulate(rmsnorm(h, p['norm_ffn'][i]), sh2, sc2)
        y, f_new = conv_ffn(xn, ffn0[i], p['ffn_w_up'][i], p['ffn_conv_w'][i], p['ffn_conv_b'][i],
                            p['ffn_w_down'][i])
        ffn_out.append(f_new)
        h = h + g2[:, None] * y
        if i == N_A_LAYERS - 1:
            ksh, ksc = jnp.split(cs @ p['kv_ada_w'] + p['kv_ada_b'], 2, axis=-1)
            xkv = modulate(rmsnorm(h, p['kv_norm']), ksh, ksc)
            kvp = (xkv @ p['w_kv']).reshape(B_, T, 2, ATT_HEADS, HEAD_DIM)
            k_new = rope(kvp[:, :, 0], pos)
            v_new = kvp[:, :, 1]
            if kv_past is None:
                kv = (k_new, v_new, None)
            else:
                ck, cv = kv_past
                kv = (jnp.concatenate([ck.astype(k_new.dtype), k_new], axis=1),
                      jnp.concatenate([cv.astype(v_new.dtype), v_new], axis=1), ck.shape[1])
    y = rmsnorm(h, p['final_norm'])
    return y, jnp.stack(ssm_out), jnp.stack(conv_out), jnp.stack(ffn_out), k_new, v_new


def setup_inputs(seed: int = 0) -> dict:
    key = jax.random.key(seed)
    keys = iter(jax.random.split(key, 48))
    f32 = jnp.float32

    def nrm(shape, scale):
        return jax.random.normal(next(keys), shape, f32) * scale

    def gain(shape):
        return 1.0 + nrm(shape, 0.01)

    win_buf = min(WINDOW_MAX, PAST_LEN)
    D = D_MODEL
    x_prompt = nrm((BATCH, SEQ, D), 1.0)
    x_sample = nrm((DEC_BATCH, DEC_SEQ, D), 1.0)
    state_ssm = nrm((N_A_LAYERS, DEC_BATCH, M_HEADS, M_HEADDIM, M_D_STATE), 0.1)
    state_conv = nrm((N_A_LAYERS, DEC_BATCH, M_CONV - 1, M_CONV_DIM), 1.0)
    state_ffn_conv = nrm((DEPTH, DEC_BATCH, FFN_CONV - 1, 2 * D_FF), 1.0)
    cache_k = nrm((DEC_BATCH, win_buf, ATT_HEADS, HEAD_DIM), 1.0)
    cache_v = nrm((DEC_BATCH, win_buf, ATT_HEADS, HEAD_DIM), 1.0)
    c_prompt = nrm((BATCH, D), 1.0)
    c_sample = nrm((DEC_BATCH, D), 1.0)
    ada_w = nrm((DEPTH, D, 6 * D), 0.5 * D ** -0.5)
    ada_b = nrm((DEPTH, 6 * D), 0.02)
    norm_mix = gain((DEPTH, D))
    norm_ffn = gain((DEPTH, D))
    m_w_in = nrm((N_A_LAYERS, D, M_IN_DIM), D ** -0.5)
    m_conv_w = nrm((N_A_LAYERS, M_CONV, M_CONV_DIM), 0.5)
    m_conv_b = nrm((N_A_LAYERS, M_CONV_DIM), 0.02)
    dt0 = jnp.exp(jax.random.uniform(next(keys), (N_A_LAYERS, M_HEADS), f32,
                                     math.log(1e-3), math.log(1e-1)))
    m_dt_bias = dt0 + jnp.log(-jnp.expm1(-dt0))
    m_A_log = jnp.log(jax.random.uniform(next(keys), (N_A_LAYERS, M_HEADS), f32, 1.0, 16.0))
    m_D = 1.0 + nrm((N_A_LAYERS, M_HEADS), 0.1)
    m_norm = gain((N_A_LAYERS, M_D_INNER))
    m_w_out = nrm((N_A_LAYERS, M_D_INNER, D), M_D_INNER ** -0.5)
    kv_norm = gain((D,))
    kv_ada_w = nrm((D, 2 * D), 0.5 * D ** -0.5)
    kv_ada_b = nrm((2 * D,), 0.02)
    w_kv = nrm((D, 2 * ATT_WIDTH), D ** -0.5)
    w_q = nrm((N_B_LAYERS, D, ATT_WIDTH), D ** -0.5)
    w_o = nrm((N_B_LAYERS, ATT_OUT, D), ATT_OUT ** -0.5)
    ffn_w_up = nrm((DEPTH, D, 2 * D_FF), D ** -0.5)
    ffn_conv_w = nrm((DEPTH, FFN_CONV, 2 * D_FF), 0.6)
    ffn_conv_b = nrm((DEPTH, 2 * D_FF), 0.02)
    ffn_w_down = nrm((DEPTH, D_FF, D), D_FF ** -0.5)
    final_norm = gain((D,))
    return {'x_prompt': x_prompt, 'x_sample': x_sample, 'state_ssm': state_ssm, 'state_conv': state_conv,
            'state_ffn_conv': state_ffn_conv, 'cache_k': cache_k, 'cache_v': cache_v,
            'c_prompt': c_prompt, 'c_sample': c_sample, 'ada_w': ada_w, 'ada_b': ada_b,
            'norm_mix': norm_mix, 'norm_ffn': norm_ffn, 'm_w_in': m_w_in, 'm_conv_w': m_conv_w,
            'm_conv_b': m_conv_b, 'm_dt_bias': m_dt_bias, 'm_A_log': m_A_log, 'm_D': m_D,
            'm_norm': m_norm, 'm_w_out': m_w_out, 'kv_norm': kv_norm, 'kv_ada_w': kv_ada_w,
            'kv_ada_b': kv_ada_b, 'w_kv': w_kv, 'w_q': w_q, 'w_o': w_o, 'ffn_w_up': ffn_w_up,
            'ffn_conv_w': ffn_conv_w, 'ffn_conv_b': ffn_conv_b, 'ffn_w_down': ffn_w_down,
            'final_norm': final_norm}


def reference(x_prompt, x_sample, state_ssm, state_conv, state_ffn_conv, cache_k, cache_v,
              c_prompt, c_sample, ada_w, ada_b, norm_mix, norm_ffn, m_w_in, m_conv_w, m_conv_b,
              m_dt_bias, m_A_log, m_D, m_norm, m_w_out, kv_norm, kv_ada_w, kv_ada_b, w_kv, w_q, w_o,
              ffn_w_up, ffn_conv_w, ffn_conv_b, ffn_w_down, final_norm):
    p = dict(ada_w=ada_w, ada_b=ada_b, norm_mix=norm_mix, norm_ffn=norm_ffn, m_w_in=m_w_in,
             m_conv_w=m_conv_w, m_conv_b=m_conv_b, m_dt_bias=m_dt_bias, m_A_log=m_A_log, m_D=m_D,
             m_norm=m_norm, m_w_out=m_w_out, kv_norm=kv_norm, kv_ada_w=kv_ada_w, kv_ada_b=kv_ada_b,
             w_kv=w_kv, w_q=w_q, w_o=w_o, ffn_w_up=ffn_w_up, ffn_conv_w=ffn_conv_w,
             ffn_conv_b=ffn_conv_b, ffn_w_down=ffn_w_down, final_norm=final_norm)
    Bp, S, _ = x_prompt.shape
    Bs, T, _ = x_sample.shape
    ssm0 = jnp.zeros((N_A_LAYERS, Bp, M_HEADS, M_HEADDIM, M_D_STATE), state_ssm.dtype)
    conv0 = jnp.zeros((N_A_LAYERS, Bp, M_CONV - 1, M_CONV_DIM), x_prompt.dtype)
    ffn0 = jnp.zeros((DEPTH, Bp, FFN_CONV - 1, 2 * D_FF), x_prompt.dtype)
    y_p, ssm_p, conv_p, ffn_p, k_p, v_p = trunk(x_prompt, c_prompt, jnp.arange(S, dtype=jnp.int32),
                                                ssm0, conv0, ffn0, None, p)
    keep = min(WINDOW_MAX, S)
    k_p = k_p[:, S - keep:]
    v_p = v_p[:, S - keep:]
    pos_s = PAST_LEN + jnp.arange(T, dtype=jnp.int32)
    y_s, ssm_s, conv_s, ffn_s, k_s, v_s = trunk(x_sample, c_sample, pos_s, state_ssm, state_conv,
                                                state_ffn_conv, (cache_k, cache_v), p)
    return (y_p, y_s, ssm_p, ssm_s, conv_p, conv_s, ffn_p, ffn_s, k_p, k_s, v_p, v_s)
```

```python
import numpy as np
import concourse.bass as bass
import concourse.mybir as mybir
from concourse.bass_utils import run_bass_kernel_spmd

F32 = mybir.dt.float32
BF16 = mybir.dt.bfloat16
I32 = mybir.dt.int32
ALU = mybir.AluOpType
AF = mybir.ActivationFunctionType

D = 1024
SEQ = 8192
TILE = 512
NTILES = SEQ // TILE
NSB = 4
NST = 8
NS = NSB * NST
EPS = 1e-6
NEG = -30000.0
SPAN = 2048
DIL = ((128, 1), (512, 4), (2048, 16))
PADC = 2176


class V:
    def __init__(s, ap, t):
        s.ap = ap
        s.t = t

    def __getitem__(s, k):
        return V(s.ap[k], s.t)

    def re(self, pat, **kw):
        return V(self.ap.rearrange(pat, **kw), self.t)

    def bc(s, axis, shape):
        return V(s.ap.unsqueeze(axis).to_broadcast(shape), s.t)

    def bitcast(s, dt):
        return V(s.ap.bitcast(dt), s.t)


class T:
    def __init__(s, h, dram=False):
        s.h = h
        s.w = None
        s.r = {}
        s.dsem = {}
        s.dcnt = {}
        s.dram = dram
        s.multi = dram
        s.ring_i = 0
        s.wl = {}
        s.psum = False

    def __getitem__(s, k):
        return V(s.h[k], s)

    def v(s):
        return V(s.h[:] if not s.dram else s.h, s)


class Ring:
    def __init__(s, items):
        s.items = items
        s.i = 0

    def next(s):
        x = s.items[s.i % len(s.items)]
        s.i += 1
        return x


class KB:
    def __init__(s, nc, es):
        s.nc = nc
        s.es = es
        s.E = {'pe': nc.tensor, 'act': nc.scalar, 'dve': nc.vector, 'pool': nc.gpsimd, 'sp': nc.sync}
        s.esem = {k: es.enter_context(nc.semaphore("e_" + k)) for k in ('pe', 'act', 'dve', 'pool')}
        s.cnt = {k: 0 for k in s.esem}
        s.waited = {k: {} for k in s.E}
        s.semobj = {}
        s.allsem = {}
        s.nsb = 0

    def sb(s, shape, dt, name=None):
        s.nsb += 1
        h = s.es.enter_context(s.nc.sbuf_tensor(name or ("t%d" % s.nsb), list(shape), dt))
        return T(h)

    def dr(s, name, shape, dt, kind="Internal"):
        h = s.nc.dram_tensor(name, list(shape), dt, kind=kind).ap()
        return T(h, dram=True)

    def _waits(s, eng, reads, writes):
        need = {}

        def add(st):
            if st is None:
                return
            sem, val = st
            if need.get(sem.name, (None, 0))[1] < val:
                need[sem.name] = (sem, val)
        for t in reads:
            if t is not None:
                add(t.w)
                for st in t.wl.values():
                    add(st)
                if t.psum:
                    for st in t.r.values():
                        if eng not in s.esem or st[0] is not s.esem[eng]:
                            add(st)
        for t in writes:
            if t is not None:
                add(t.w)
                for st in t.wl.values():
                    add(st)
                for st in t.r.values():
                    add(st)
        for nm, (sem, val) in need.items():
            if eng == 'pe' and sem is s.esem['pe']:
                continue
            if s.waited[eng].get(nm, 0) >= val:
                continue
            s.E[eng].wait_ge(sem, val)
            s.waited[eng][nm] = val

    def _stamp(s, st, reads, writes):
        for t in writes:
            if t is not None:
                t.w = st
                t.r = {}
        for t in reads:
            if t is not None and t not in writes:
                t.r[st[0].name] = st

    def emit(s, eng, fn, reads, writes, inc=True):
        reads = [v.t for v in reads if isinstance(v, V)]
        writes = [v.t for v in writes if isinstance(v, V)]
        s._waits(eng, reads, writes)
        inst = fn(s.E[eng])
        if inc:
            s.cnt[eng] += 1
            inst.then_inc(s.esem[eng], 1)
            s._stamp((s.esem[eng], s.cnt[eng]), reads, writes)
        else:
            s._stamp((s.esem[eng], s.cnt[eng] + 1), reads, writes)

    def dma(s, q, out, in_, sem_t=None):
        owner = out.t
        reads = [in_.t] if in_.t is not None else []
        writes = [out.t] if out.t is not None else []
        if owner.multi:
            s._waits(q, reads, [])
            key = (q, owner.ring_i % 4)
            owner.ring_i += 1
        else:
            s._waits(q, reads, writes)
            key = (q, 0)
        if key not in owner.dsem:
            owner.dsem[key] = s.es.enter_context(s.nc.semaphore("d%d" % len(s.semobj)))
            s.semobj[owner.dsem[key].name] = owner.dsem[key]
            owner.dcnt[key] = 0
        sem = owner.dsem[key]
        if owner.multi and owner.dcnt[key] > 0 and s.waited[q].get(sem.name, 0) < owner.dcnt[key]:
            s.E[q].wait_ge(sem, owner.dcnt[key])
            s.waited[q][sem.name] = owner.dcnt[key]
        s.E[q].dma_start(out=out.ap, in_=in_.ap).then_inc(sem, 16)
        owner.dcnt[key] += 16
        st = (sem, owner.dcnt[key])
        s.allsem[sem.name] = st
        if owner.multi:
            owner.wl[sem.name] = st
            for t in reads:
                t.r[sem.name] = st
        else:
            s._stamp(st, reads, writes)

    def mm(s, out, l, r, start=True, stop=True):
        s.emit('pe', lambda e: e.matmul(out.ap, lhsT=l.ap, rhs=r.ap, start=start, stop=stop), [l, r], [out])

    def tr(s, out, in_, ident):
        s.emit('pe', lambda e: e.transpose(out.ap, in_.ap, ident.ap), [in_, ident], [out])

    def act(s, out, in_, func, bias=None, scale=1.0):
        rd = [in_] + ([bias] if isinstance(bias, V) else [])
        kw = {}
        if bias is not None:
            kw['bias'] = bias.ap if isinstance(bias, V) else bias
        sc = scale.ap if isinstance(scale, V) else scale
        if isinstance(scale, V):
            rd.append(scale)
        s.emit('act', lambda e: e.activation(out=out.ap, in_=in_.ap, func=func, scale=sc, **kw), rd, [out])

    def tt(s, out, a, b, op, eng='dve'):
        s.emit(eng, lambda e: e.tensor_tensor(out=out.ap, in0=a.ap, in1=b.ap, op=op), [a, b], [out])

    def ts(s, out, a, s1, s2, op0, op1=None, eng='dve'):
        rd = [a] + [x for x in (s1, s2) if isinstance(x, V)]
        a1 = s1.ap if isinstance(s1, V) else s1
        a2 = s2.ap if isinstance(s2, V) else s2
        if op1 is None:
            s.emit(eng, lambda e: e.tensor_scalar(out=out.ap, in0=a.ap, scalar1=a1, scalar2=None, op0=op0), rd, [out])
        else:
            s.emit(eng, lambda e: e.tensor_scalar(out=out.ap, in0=a.ap, scalar1=a1, scalar2=a2, op0=op0, op1=op1), rd, [out])

    def stt(s, out, a, sc, b, op0, op1, eng='dve'):
        rd = [a, b] + ([sc] if isinstance(sc, V) else [])
        a1 = sc.ap if isinstance(sc, V) else sc
        s.emit(eng, lambda e: e.scalar_tensor_tensor(out=out.ap, in0=a.ap, scalar=a1, in1=b.ap, op0=op0, op1=op1), rd, [out])

    def cp(s, out, a, eng='dve'):
        if eng == 'act':
            s.act(out, a, AF.Copy)
        else:
            s.emit(eng, lambda e: e.tensor_copy(out=out.ap, in_=a.ap), [a], [out])

    def memset(s, out, val, eng='dve'):
        s.emit(eng, lambda e: e.memset(out.ap, val), [], [out])

    def recip(s, out, a):
        s.emit('dve', lambda e: e.reciprocal(out=out.ap, in_=a.ap), [a], [out])

    def finish(s, tiles):
        for (sem, val) in s.allsem.values():
            s.E['sp'].wait_ge(sem, val)
        for k in ('pe', 'act', 'dve'):
            if s.cnt[k] > 0:
                s.E['sp'].wait_ge(s.esem[k], s.cnt[k])


NB_LOC = 33
NDEL = (2, 5, 17)
MIDX0 = (0, 2, 7)
NCACHE = (1, 4, 16)
SM0 = (0, 1, 5)


def build_program():
    from contextlib import ExitStack
    nc = bass.Bass("TRN2", target_bir_lowering=False)
    es = ExitStack()
    kb = KB(nc, es)

    def din(name, shape, dt=F32):
        return V(nc.dram_tensor(name, list(shape), dt, kind="ExternalInput").ap(), None)

    def dout(name, shape, dt=F32):
        return kb.dr(name, shape, dt, kind="ExternalOutput")

    xp = din("xp", [128, 8, SEQ]); xs = din("xs", [128, 8, NS]); csin = din("cs", [128, 8, 5])
    pk = din("pk", [128, PK_N])
    c_if = din("c_if", [128, 128]); c_ones = din("c_ones", [128, 128]); c_tri = din("c_tri", [128, 128])
    c_sel = din("c_sel", [32, 32 * 128]); c_exp = din("c_exp", [32, 16 * 128]); c_rot = din("c_rot", [128, 128])
    c_par = din("c_par", [128, 72]); c_l64 = din("c_l64", [128, 64]); c_neg = din("c_neg", [128, 128])
    rk = din("rk", [1, 4], I32)
    ropek = din("ropek", [128, 2, NB_LOC * 128]); ropes = din("ropes", [128, 2, NS])
    amask = din("amask", [128, 24 * 128]); smask = din("smask", [128, 21 * 128]); smaskn = din("smaskn", [NST, 3 * 128])
    st_ssm = din("st_ssm", [2, NSB, 128, 2048])
    st_conv = din("st_conv", [128, 2 * NSB * 24 * 3]); st_ffn = din("st_ffn", [128, 4 * NSB * 44 * 2])
    ckT = [din("ck%d" % g, [NSB, 8, 128, DIL[g][0]]) for g in range(3)]
    cvv = [din("cv%d" % g, [NSB, DIL[g][0], 1024]) for g in range(3)]
    ada_w = din("ada_w", [4, D, 6 * D]); kv_ada_w = din("kv_ada_w", [D, 2 * D])
    m_w_in = din("m_w_in", [2, D, 5152]); m_w_out = din("m_w_out", [2, 2048, D])
    w_kv = din("w_kv", [D, 6144]); w_q = din("w_q", [2, D, 3072]); w_o = din("w_o", [2, D, D])
    f_up = din("ffn_w_up", [4, D, 5632]); f_dn = din("ffn_w_down", [4, 2816, D])
    o_yp = dout("o_yp", [128, 8, SPAN]); o_ys = dout("o_ys", [128, 8, NS])
    o_ssmp = dout("o_ssmp", [2, 128, 2048]); o_ssms = dout("o_ssms", [2, NSB, 128, 2048])
    o_convp = dout("o_convp", [128, 2 * 24 * 3]); o_convs = dout("o_convs", [128, 2 * NSB * 24 * 3])
    o_ffnp = dout("o_ffnp", [128, 4 * 44 * 2]); o_ffns = dout("o_ffns", [128, 4 * NSB * 44 * 2])
    o_kp = dout("o_kp", [128, 24, SPAN]); o_ks = dout("o_ks", [128, 24, NS])
    o_vp = dout("o_vp", [16, 128, 3072]); o_vs = dout("o_vs", [NSB, NST, 3072])
    outs = [o_yp, o_ys, o_ssmp, o_ssms, o_convp, o_convs, o_ffnp, o_ffns, o_kp, o_ks, o_vp, o_vs]
    h1s = kb.dr("h1s", [(PADC + SEQ) // 128, 128, 8, 128], F32)
    kts = kb.dr("kts", [128, 24, NB_LOC * 128], BF16)
    vts = kb.dr("vts", [NB_LOC, 128, 3072], BF16)
    vns = kb.dr("vns", [NSB, NST, 3072], BF16)

    sb = kb.sb
    identf = sb([128, 128], F32); onesf = sb([128, 128], F32); trif = sb([128, 128], F32)
    identb = sb([128, 128], BF16); trib = sb([128, 128], BF16); rotb = sb([128, 128], BF16); negb = sb([128, 128], BF16)
    selb = sb([32, 32 * 128], BF16); expb = sb([32, 16 * 128], BF16)
    par = sb([128, 72], F32); l64 = sb([128, 64], F32)
    pkt = sb([128, PK_N], F32)
    cst = sb([128, 8, 5], F32); csb = sb([128, 8, 5], BF16)
    mods = sb([128, 48, 5], F32)
    GM = sb([128, 9, 8, 5], F32); SHM = sb([128, 9, 8, 5], F32); GTM = sb([128, 8, 8, 5], F32)
    zerob = sb([128, 512], BF16)
    for t, src in [(identf, c_if), (onesf, c_ones), (trif, c_tri), (par, c_par), (pkt, pk), (cst, csin), (l64, c_l64)]:
        kb.dma('sp', t.v(), src)
    for t, src in [(identb, c_if), (trib, c_tri), (rotb, c_rot), (selb, c_sel), (expb, c_exp), (negb, c_neg)]:
        kb.dma('pool', t.v(), src)
    kb.memset(zerob.v(), 0.0)
    kb.act(csb.v(), cst.v(), AF.Silu)

    def P(name, li=None):
        o, n = PK_OFF[name if li is None else (name, li)]
        return pkt[:, o:o + n]

    PSall = [T(es.enter_context(nc.psum_tensor("ps%d" % i, [128, 512], F32))) for i in range(8)]
    for t_ in PSall:
        t_.psum = True
    PS = Ring(PSall)
    WB = Ring([sb([128, 4096], BF16) for _ in range(2)])

    def wload(wd, kcn, f0, fw, npart=128):
        wb = WB.next()
        view = V(wb.h[0:npart, 0:kcn * fw].rearrange("p (k f) -> p k f", f=fw), wb)
        kb.dma('pool', view, V(wd.ap[:, :, f0:f0 + fw], None))
        return view

    def linear(wd, kcn, F0, F, gw, rhs_fn, NT, evac, npart=128):
        ng = -(-F // gw)
        nxt = wload(wd, kcn, F0, min(gw, F), npart)
        for g in range(ng):
            wv = nxt
            if g + 1 < ng:
                nxt = wload(wd, kcn, F0 + (g + 1) * gw, min(gw, F - (g + 1) * gw), npart)
            fw = min(gw, F - g * gw)
            for jj in range(-(-fw // 128)):
                w = min(128, fw - jj * 128)
                ps = PS.next()
                for kc in range(kcn):
                    kb.mm(ps[0:w, 0:NT], wv[:, kc, jj * 128:jj * 128 + w], rhs_fn(kc), kc == 0, kc == kcn - 1)
                evac(g * (gw // 128) + jj, ps, w)

    def wview(w3, li, p=128):
        a = w3.ap if li is None else w3.ap[li]
        return V(a.rearrange("(k p) f -> p k f", p=p), None)

    for li in range(4):
        def ev(j, ps, w, li=li):
            kb.ts(mods[:, j, :], ps[:, 0:5], P('ada_b', li)[:, j:j + 1], None, ALU.add)
        linear(wview(ada_w, li), 8, 0, 6 * D, 512, lambda kc: csb[:, kc, :], 5, ev)
        for sub in range(2):
            k = 2 * li + sub
            nrm = P('norm_mix' if sub == 0 else 'norm_ffn', li)
            b0 = 24 * sub
            for kc in range(8):
                kb.ts(GM[:, k, kc, :], mods[:, b0 + 8 + kc, :], 1.0, nrm[:, kc:kc + 1], ALU.add, ALU.mult)
            kb.cp(SHM[:, k, :, :], mods[:, b0:b0 + 8, :])
            kb.cp(GTM[:, k, :, :], mods[:, b0 + 16:b0 + 24, :])

    def evkv(j, ps, w):
        kb.ts(mods[:, j, :], ps[:, 0:5], P('kv_ada_b')[:, j:j + 1], None, ALU.add)
    linear(wview(kv_ada_w, None), 8, 0, 2 * D, 512, lambda kc: csb[:, kc, :], 5, evkv)
    for kc in range(8):
        kb.ts(GM[:, 8, kc, :], mods[:, 8 + kc, :], 1.0, P('kv_norm')[:, kc:kc + 1], ALU.add, ALU.mult)
    kb.cp(SHM[:, 8, :, :], mods[:, 0:8, :])

    hT = sb([128, 8, TILE], F32); hS = sb([128, 8, NS], F32)
    xn = sb([128, 8, TILE], BF16)
    big = sb([128, 40, TILE], BF16)
    yn = sb([128, 16, TILE], BF16)
    FR = Ring([sb([128, TILE], F32) for _ in range(5)])
    rtile = sb([128, TILE], F32)
    prec = Ring([sb([128, NSB * 11 + TILE], F32) for _ in range(3)])
    ropet = sb([128, 2, TILE], F32)
    negA = sb([32, 2], F32)
    HTp = sb([128, 2, 2048], F32)
    HTb = sb([128, 2048], BF16)
    convc = sb([128, 2 * 24 * 3], F32); ffnc = sb([128, 4 * 44 * 2], F32)
    convcs = sb([128, 2 * NSB * 24 * 3], F32); ffncs = sb([128, 4 * NSB * 44 * 2], F32)
    adt = sb([128, 128], F32); acs = sb([128, 64 + 128], F32); w2 = sb([128, 96], F32)
    achl = sb([32, 2, 128], BF16)
    xTt = Ring([sb([128, 512], BF16) for _ in range(2)])
    xdt0 = sb([128, 2048], BF16); xdt1 = sb([128, 2048], BF16); xdd = sb([128, 2048], BF16)
    btm = sb([128, 512], BF16); cbm = sb([128, 512], BF16)
    mhs = Ring([sb([128, 512], BF16) for _ in range(3)])
    vsl = [sb([128, 16, 65], BF16) for _ in range(3)]
    kns = sb([128, 24, NS], BF16)
    dtT = ropet[0:32, 0, :]; aT = ropet[0:32, 1, :]
    cc4 = convc.v().re("p (l j k) -> p l j k", l=2, j=24)
    fc4 = ffnc.v().re("p (l j k) -> p l j k", l=4, j=44)
    ccs5 = convcs.v().re("p (l s j k) -> p l s j k", l=2, s=NSB, j=24)
    fcs5 = ffncs.v().re("p (l s j k) -> p l s j k", l=4, s=NSB, j=44)
    kb.dma('sp', convcs.v(), st_conv); kb.dma('sp', ffncs.v(), st_ffn)
    kb.memset(convc.v(), 0.0); kb.memset(ffnc.v(), 0.0)
    kb.memset(HTp.v(), 0.0)
    for li in range(2):
        kb.act(negA[:, li:li + 1], P('A_log', li)[0:32, 0:1], AF.Exp)
        kb.ts(negA[:, li:li + 1], negA[:, li:li + 1], -1.0, None, ALU.mult)
    zs = big; xbc = big; gbuf = big

    def rms_mod(h, NT, k, segs, gcol, out):
        ps = PS.next()
        for kc in range(8):
            sq = FR.next()
            kb.act(sq[:, 0:NT], h[:, kc, 0:NT], AF.Square)
            kb.mm(ps[:, 0:NT], onesf.v(), sq[:, 0:NT], kc == 0, kc == 7)
        r = rtile
        kb.ts(r[:, 0:NT], ps[:, 0:NT], 1.0 / D, EPS, ALU.mult, ALU.add)
        kb.act(r[:, 0:NT], r[:, 0:NT], AF.Sqrt)
        kb.recip(r[:, 0:NT], r[:, 0:NT])
        for kc in range(8):
            t = FR.next()
            kb.tt(t[:, 0:NT], h[:, kc, 0:NT], r[:, 0:NT], ALU.mult)
            for (c0, ln, mc) in segs:
                if gcol is None:
                    kb.ts(out[:, kc, c0:c0 + ln], t[:, c0:c0 + ln], GM[:, k, kc, mc:mc + 1], SHM[:, k, kc, mc:mc + 1], ALU.mult, ALU.add)
                else:
                    kb.ts(out[:, kc, c0:c0 + ln], t[:, c0:c0 + ln], gcol[:, kc:kc + 1], None, ALU.mult)

    def conv_chunk(ps, NT, segs, K, wcol, bcol, carry_fn, func, out_v):
        pc = prec.next()
        L = segs[0][1]
        nseg = len(segs)
        pv = pc[:, 0:nseg * (K - 1 + L)].re("p (s l) -> p s l", l=K - 1 + L)
        for si in range(nseg):
            kb.cp(pv[:, si, 0:K - 1], carry_fn(si))
        kb.cp(pv[:, :, K - 1:K - 1 + L], ps[:, 0:NT].re("p (s l) -> p s l", l=L), eng='act')
        for si in range(nseg):
            kb.cp(carry_fn(si), pv[:, si, L:L + K - 1])
        ac = FR.next()
        av = ac[:, 0:NT].re("p (s l) -> p s l", l=L)
        kb.act(av, pv[:, :, 0:L], AF.Copy, scale=wcol(0))
        for k in range(1, K):
            kb.stt(av, pv[:, :, k:k + L], wcol(k), av, ALU.mult, ALU.add)
        kb.act(out_v, ac[:, 0:NT], func, bias=bcol)

    def ssd_chunk(li, c0, Q, HT):
        ps = PS.next()
        kb.mm(ps[0:Q, 0:32], aT[0:32, c0:c0 + Q], identf[0:32, 0:32])
        kb.mm(ps[0:Q, 32:64], dtT[0:32, c0:c0 + Q], identf[0:32, 0:32])
        kb.cp(adt[0:Q, 0:64], ps[0:Q, 0:64])
        kb.tt(adt[0:Q, 64:96], adt[0:Q, 32:64], par[0:Q, 0:32], ALU.mult)
        kb.tt(adt[0:Q, 96:128], adt[0:Q, 32:64], par[0:Q, 32:64], ALU.mult)
        ps2 = PS.next()
        kb.mm(ps2[0:Q, 0:32], trif[0:Q, 0:Q], adt[0:Q, 0:32])
        kb.mm(ps2[:, 32:64], onesf[0:Q, :], adt[0:Q, 0:32])
        kb.mm(ps2[0:32, 64:64 + Q], adt[0:Q, 0:32], trif[0:Q, 0:Q])
        kb.cp(acs[0:Q, 0:32], ps2[0:Q, 0:32])
        kb.cp(acs[:, 32:64], ps2[:, 32:64])
        kb.cp(acs[0:32, 64:64 + Q], ps2[0:32, 64:64 + Q])
        kb.cp(achl[:, 0, 0:Q], acs[0:32, 64:64 + Q])
        kb.tt(achl[:, 1, 0:Q], acs[0:32, 64:64 + Q], achl[:, 0, 0:Q], ALU.subtract)
        kb.tt(w2[0:Q, 0:32], acs[0:Q, 32:64], acs[0:Q, 0:32], ALU.subtract)
        kb.act(w2[0:Q, 0:32], w2[0:Q, 0:32], AF.Exp)
        kb.tt(w2[0:Q, 0:32], w2[0:Q, 0:32], adt[0:Q, 32:64], ALU.mult)
        kb.act(w2[:, 32:64], acs[:, 32:64], AF.Exp)
        for grp in range(4):
            psb = PS.next()
            pb = psb.v().bitcast(BF16)
            for j4 in range(4):
                kb.tr(pb[0:Q, j4 * 128:(j4 + 1) * 128], xbc[:, 16 + grp * 4 + j4, c0:c0 + Q], identb.v())
            xT = xTt.next()
            kb.cp(xT[0:Q, :], pb[0:Q, 0:512], eng='act')
            x3 = xT[0:Q, :].re("q (h p) -> q h p", p=64)
            sl = slice(grp * 512, (grp + 1) * 512)
            kb.tt(xdt0[0:Q, sl].re("q (h p) -> q h p", p=64), x3, adt[0:Q, 64 + grp * 8:72 + grp * 8].bc(2, [Q, 8, 64]), ALU.mult)
            kb.tt(xdt1[0:Q, sl].re("q (h p) -> q h p", p=64), x3, adt[0:Q, 96 + grp * 8:104 + grp * 8].bc(2, [Q, 8, 64]), ALU.mult)
            kb.tt(xdd[0:Q, sl].re("q (h p) -> q h p", p=64), x3, w2[0:Q, grp * 8:grp * 8 + 8].bc(2, [Q, 8, 64]), ALU.mult)
        psb = PS.next()
        pb = psb.v().bitcast(BF16)
        for g in range(4):
            kb.tr(pb[0:Q, g * 128:(g + 1) * 128], xbc[:, 32 + g, c0:c0 + Q], identb.v())
        kb.cp(btm[0:Q, :], pb[0:Q, 0:512], eng='act')
        ps = PS.next()
        for g in range(4):
            kb.mm(ps[0:Q, g * Q:(g + 1) * Q], xbc[:, 32 + g, c0:c0 + Q], xbc[:, 36 + g, c0:c0 + Q])
        kb.tt(cbm[0:Q, 0:4 * Q].re("q (g l) -> q g l", l=Q), ps[0:Q, 0:4 * Q].re("q (g l) -> q g l", l=Q),
              trib[0:Q, 0:Q].bc(1, [Q, 4, Q]), ALU.mult)
        for bt in range(4):
            mh = {}
            for hb in (2 * bt, 2 * bt + 1):
                ps = PS.next()
                for e in range(4):
                    h = 4 * hb + e
                    kb.mm(ps[0:Q, e * Q:(e + 1) * Q], selb[0:32, h * 128:h * 128 + Q], achl[:, 0, 0:Q], True, False)
                    kb.mm(ps[0:Q, e * Q:(e + 1) * Q], selb[0:32, h * 128:h * 128 + Q], achl[:, 1, 0:Q], False, True)
                d = FR.next()
                for e in range(4):
                    h = 4 * hb + e
                    kb.ts(d[0:Q, e * Q:(e + 1) * Q], ps[0:Q, e * Q:(e + 1) * Q], acs[0:Q, h:h + 1], 0.0, ALU.subtract, ALU.min)
                m = mhs.next()
                kb.act(m[0:Q, 0:4 * Q], d[0:Q, 0:4 * Q], AF.Exp)
                kb.tt(m[0:Q, 0:4 * Q].re("q (e l) -> q e l", l=Q), m[0:Q, 0:4 * Q].re("q (e l) -> q e l", l=Q),
                      cbm[0:Q, bt * Q:(bt + 1) * Q].bc(1, [Q, 4, Q]), ALU.mult)
                mh[hb] = m
            psY = PS.next(); psO = PS.next(); psE = PS.next()
            for pr in range(4):
                pair = 4 * bt + pr
                for e in range(2):
                    h = 2 * pair + e
                    xd = xdt0 if e == 0 else xdt1
                    kb.mm(psY[:, pr * Q:(pr + 1) * Q], xd[0:Q, pair * 128:(pair + 1) * 128],
                          mh[h // 4][0:Q, (h % 4) * Q:(h % 4 + 1) * Q], e == 0, e == 1)
                kb.mm(psO[:, pr * Q:(pr + 1) * Q], HTb[:, pair * 128:(pair + 1) * 128], xbc[:, 36 + bt, c0:c0 + Q])
                kb.mm(psE[:, pr * Q:(pr + 1) * Q], expb[0:32, pair * 128:(pair + 1) * 128], achl[:, 0, 0:Q], True, False)
                kb.mm(psE[:, pr * Q:(pr + 1) * Q], expb[0:32, pair * 128:(pair + 1) * 128], achl[:, 1, 0:Q], False, True)
            et = FR.next(); yt = FR.next()
            kb.act(et[:, 0:4 * Q], psE[:, 0:4 * Q], AF.Exp)
            kb.tt(et[:, 0:4 * Q], psO[:, 0:4 * Q], et[:, 0:4 * Q], ALU.mult)
            kb.tt(yt[:, 0:4 * Q], psY[:, 0:4 * Q], et[:, 0:4 * Q], ALU.add)
            for pr in range(4):
                pair = 4 * bt + pr
                kb.stt(yt[:, pr * Q:(pr + 1) * Q], xbc[:, 16 + pair, c0:c0 + Q], P('m_D', li)[:, pair:pair + 1],
                       yt[:, pr * Q:(pr + 1) * Q], ALU.mult, ALU.add)
            kb.tt(yt[:, 0:4 * Q].re("p (c l) -> p c l", l=Q), yt[:, 0:4 * Q].re("p (c l) -> p c l", l=Q),
                  zs[:, 4 * bt:4 * bt + 4, c0:c0 + Q], ALU.mult)
            sq = FR.next()
            kb.act(sq[:, 0:4 * Q], yt[:, 0:4 * Q], AF.Square)
            psN = PS.next()
            for pr in range(4):
                kb.mm(psN[:, 0:Q], onesf.v(), sq[:, pr * Q:(pr + 1) * Q], pr == 0, pr == 3)
            r = sq
            kb.ts(r[:, 0:Q], psN[:, 0:Q], 1.0 / 512, EPS, ALU.mult, ALU.add)
            kb.act(r[:, 0:Q], r[:, 0:Q], AF.Sqrt)
            kb.recip(r[:, 0:Q], r[:, 0:Q])
            for pr in range(4):
                pair = 4 * bt + pr
                kb.stt(yn[:, pair, c0:c0 + Q], yt[:, pr * Q:(pr + 1) * Q], P('m_norm', li)[:, pair:pair + 1], r[:, 0:Q],
                       ALU.mult, ALU.mult)
        for g in range(4):
            psS = PS.next()
            kb.mm(psS[:, 0:512], btm[0:Q, g * 128:(g + 1) * 128], xdd[0:Q, g * 512:(g + 1) * 512])
            hv = HT[:, g * 512:(g + 1) * 512]
            kb.tt(hv.re("n (h p) -> n h p", p=64), hv.re("n (h p) -> n h p", p=64),
                  w2[:, 32 + g * 8:40 + g * 8].bc(2, [128, 8, 64]), ALU.mult)
            kb.tt(hv, hv, psS[:, 0:512], ALU.add)
        kb.cp(HTb.v(), HT, eng='act')

    def ffn(li, h, NT, segs, sample):
        rms_mod(h, NT, 2 * li + 1, segs, None, xn)
        wc = P('ffn_conv_w', li)

        def ev_up(j, ps, w):
            cf = (lambda si: fcs5[:, li, si, j, :]) if sample else (lambda si: fc4[:, li, j, :])
            if j < 22:
                conv_chunk(ps, NT, segs, 3, lambda k: wc[:, j * 3 + k:j * 3 + k + 1], P('ffn_conv_b', li)[:, j:j + 1],
                           cf, AF.Silu, gbuf[:, j, 0:NT])
            else:
                t = FR.next()
                conv_chunk(ps, NT, segs, 3, lambda k: wc[:, j * 3 + k:j * 3 + k + 1], P('ffn_conv_b', li)[:, j:j + 1],
                           cf, AF.Identity, t[:, 0:NT])
                kb.tt(gbuf[:, j - 22, 0:NT], gbuf[:, j - 22, 0:NT], t[:, 0:NT], ALU.mult)
        linear(wview(f_up, li), 8, 0, 5632, 512, lambda kc: xn[:, kc, 0:NT], NT, ev_up)

        def ev_dn(m, ps, w):
            for (c0, ln, mc) in segs:
                kb.stt(h[:, m, c0:c0 + ln], ps[:, c0:c0 + ln], GTM[:, 2 * li + 1, m, mc:mc + 1], h[:, m, c0:c0 + ln], ALU.mult, ALU.add)
        linear(wview(f_dn, li), 22, 0, D, 128, lambda kc: gbuf[:, kc, 0:NT], NT, ev_dn)

    def a_layer(li, h, NT, segs, sample):
        rms_mod(h, NT, 2 * li, segs, None, xn)
        wc = P('m_conv_w', li)

        def ev_in(j, ps, w):
            if j < 16:
                kb.act(zs[:, j, 0:NT], ps[:, 0:NT], AF.Silu)
            elif j < 40:
                jc = j - 16
                cf = (lambda si: ccs5[:, li, si, jc, :]) if sample else (lambda si: cc4[:, li, jc, :])
                conv_chunk(ps, NT, segs, 4, lambda k: wc[:, jc * 4 + k:jc * 4 + k + 1], P('m_conv_b', li)[:, jc:jc + 1],
                           cf, AF.Silu, xbc[:, j, 0:NT])
            else:
                x = FR.next(); ax = FR.next(); mx = FR.next()
                kb.ts(x[0:32, 0:NT], ps[0:32, 0:NT], P('m_dt_bias', li)[0:32, 0:1], None, ALU.add)
                kb.act(ax[0:32, 0:NT], x[0:32, 0:NT], AF.Abs)
                kb.act(ax[0:32, 0:NT], ax[0:32, 0:NT], AF.Exp, scale=-1.0)
                kb.act(ax[0:32, 0:NT], ax[0:32, 0:NT], AF.Ln, bias=1.0)
                kb.ts(mx[0:32, 0:NT], x[0:32, 0:NT], 0.0, None, ALU.max)
                kb.tt(dtT[:, 0:NT], mx[0:32, 0:NT], ax[0:32, 0:NT], ALU.add)
                kb.ts(aT[:, 0:NT], dtT[:, 0:NT], negA[:, li:li + 1], None, ALU.mult)
        linear(wview(m_w_in, li), 8, 0, 5152, 512, lambda kc: xn[:, kc, 0:NT], NT, ev_in)
        HT = HTp[:, li, :]
        if sample:
            for si in range(NSB):
                kb.dma('sp', HT, V(st_ssm.ap[li, si], None))
                kb.cp(HTb.v(), HT, eng='act')
                ssd_chunk(li, si * NST, NST, HT)
                kb.dma('sp', V(o_ssms.h[li, si], o_ssms), HT)
        else:
            kb.cp(HTb.v(), HT, eng='act')
            for c in range(NT // 128):
                ssd_chunk(li, c * 128, 128, HT)

        def ev_out(m, ps, w):
            for (c0, ln, mc) in segs:
                kb.stt(h[:, m, c0:c0 + ln], ps[:, c0:c0 + ln], GTM[:, 2 * li, m, mc:mc + 1], h[:, m, c0:c0 + ln], ALU.mult, ALU.add)
        linear(wview(m_w_out, li), 16, 0, D, 256, lambda kc: yn[:, kc, 0:NT], NT, ev_out)
        ffn(li, h, NT, segs, sample)

    pseg = [(0, TILE, 0)]
    sseg = [(i * NST, NST, 1 + i) for i in range(NSB)]
    for ti in range(A_TILES):
        kb.dma('sp', hT.v(), V(xp.ap[:, :, ti * TILE:(ti + 1) * TILE], None))
        for li in range(2):
            a_layer(li, hT, TILE, pseg, False)
        kb.dma('sp', V(h1s.h[PADC // 128 + 4 * ti:PADC // 128 + 4 * ti + 4].rearrange("b p k c -> p k b c"), h1s),
               hT.v().re("p k (b c) -> p k b c", c=128))
    for li in range(2):
        kb.dma('sp', V(o_ssmp.h[li], o_ssmp), HTp[:, li, :])
    kb.dma('sp', o_convp.v(), convc.v())
    zero = FR.next()
    kb.memset(zero.v(), 0.0)
    for bq in range(PADC // 128):
        for hf in range(2):
            kb.dma('sp', V(h1s.h[bq][:, 4 * hf:4 * hf + 4, :], h1s), zero.v().re("p (k c) -> p k c", c=128))
    kb.dma('sp', hS.v(), xs)
    for li in range(2):
        a_layer(li, hS, NS, sseg, True)
    kb.dma('sp', o_convs.v(), convcs.v())

    if not DO_B:
        kb.dma('sp', o_ffnp.v(), ffnc.v()); kb.dma('sp', o_ffns.v(), ffncs.v())
        kb.finish(outs)
        return nc, es

    g = nc.gpsimd
    reg = es.enter_context(g.register("rkreg"))
    g.load(reg, rk.ap[0:1, 0:1])
    off = g.snap(reg)
    mview = HTp.v().re("p l n -> p (l n)").bitcast(BF16)
    am = mview[:, 0:24 * 128].re("p (t q) -> p t q", q=128)
    sm = mview[:, 24 * 128:45 * 128].re("p (t q) -> p t q", q=128)
    smn = mview[0:NST, 45 * 128:48 * 128].re("p (t q) -> p t q", q=128)
    kb.dma('pool', mview[:, 0:24 * 128], amask)
    kb.dma('pool', mview[:, 24 * 128:45 * 128], smask)
    kb.dma('pool', mview[0:NST, 45 * 128:48 * 128], smaskn)
    for v in vsl:
        kb.memset(v[:, :, 64:65], 1.0)
    KTS = Ring([xdt0, xdt1, xdd]); VSL = Ring(vsl)
    PSa = Ring(PSall[4:8])
    wkv_v = wview(w_kv, None)
    ropek_t = [None, None]

    def rope_evac(ps, NT, rc, rs, out_bf, out_f32_dma=None):
        kraw = mhs.next()
        kb.cp(kraw[:, 0:NT], ps[:, 0:NT], eng='act')
        psr = PS.next()
        kb.mm(psr[:, 0:NT], rotb.v(), kraw[:, 0:NT])
        t1 = FR.next(); t2 = FR.next()
        kb.tt(t1[:, 0:NT], ps[:, 0:NT], rc, ALU.mult)
        kb.tt(t2[:, 0:NT], psr[:, 0:NT], rs, ALU.mult)
        kb.tt(t1[:, 0:NT], t1[:, 0:NT], t2[:, 0:NT], ALU.add)
        if out_f32_dma is not None:
            kb.dma('sp', out_f32_dma, t1[:, 0:NT])
        kb.cp(out_bf, t1[:, 0:NT])

    def load_rope(c0, NT, src, col0):
        kb.dma('sp', ropet[:, :, 0:NT], V(src.ap[:, :, col0:col0 + NT], None))
        return ropet[:, 0, 0:NT], ropet[:, 1, 0:NT]

    def kv_tile(b0, nblk):
        NT = nblk * 128
        kb.dma('pool', hT[:, :, 0:NT].re("p k (b c) -> p k b c", c=128),
               V(h1s.h[bass.ds(off + b0, nblk)].rearrange("b p k c -> p k b c"), h1s))
        if KVS <= 1:
            return
        rms_mod(hT, NT, 8, [(0, NT, 0)], None, xn)
        rc, rs = load_rope(0, NT, ropek, b0 * 128)
        own = b0 >= 17
        if KVS <= 2:
            return

        def ev_k(j, ps, w):
            od = V(o_kp.h[:, j, (b0 - 17) * 128:(b0 - 17) * 128 + NT], o_kp) if own else None
            rope_evac(ps, NT, rc, rs, big[:, j, 0:NT], od)
        linear(wkv_v, 8, 0, 3072, 512, lambda kc: xn[:, kc, 0:NT], NT, ev_k)
        if KVS <= 3:
            return
        kb.dma('sp', V(kts.h[:, :, b0 * 128:b0 * 128 + NT], kts), big[:, 0:24, 0:NT])
        if KVS <= 4:
            return
        nxt = wload(wkv_v, 8, 3072, 512)
        for fg in range(6):
            wv = nxt
            if fg + 1 < 6:
                nxt = wload(wkv_v, 8, 3072 + (fg + 1) * 512, 512)
            for blk in range(nblk):
                ps = PS.next()
                for kc in range(8):
                    kb.mm(ps[:, 0:512], xn[:, kc, blk * 128:(blk + 1) * 128], wv[:, kc, :], kc == 0, kc == 7)
                vf = FR.next()
                kb.cp(vf.v(), ps.v(), eng='act')
                if own:
                    kb.dma('sp', V(o_vp.h[b0 - 17 + blk][:, fg * 512:(fg + 1) * 512], o_vp), vf.v())
                vb = mhs.next()
                kb.cp(vb.v(), vf.v())
                kb.dma('sp', V(vts.h[b0 + blk][:, fg * 512:(fg + 1) * 512], vts), vb.v())

    def early():
        kb.dma('sp', o_ffnp.v(), ffnc.v()); kb.dma('sp', o_ffns.v(), ffncs.v())
        kb.finish(outs)
        return nc, es
    if B_STAGE == 0:
        return early()
    kv_tile(0, 1)
    if B_STAGE == 1:
        return early()
    for t in range(8):
        kv_tile(1 + 4 * t, 4)
    if B_STAGE == 2:
        return early()

    rms_mod(hS, NS, 8, sseg, None, xn)
    rcs, rss = load_rope(0, NS, ropes, 0)

    def ev_ks(j, ps, w):
        rope_evac(ps, NS, rcs, rss, kns[:, j, :], V(o_ks.h[:, j, :], o_ks))
    linear(wkv_v, 8, 0, 3072, 512, lambda kc: xn[:, kc, 0:NS], NS, ev_ks)
    nxt = wload(wkv_v, 8, 3072, 512)
    for fg in range(6):
        wv = nxt
        if fg + 1 < 6:
            nxt = wload(wkv_v, 8, 3072 + (fg + 1) * 512, 512)
        for si in range(NSB):
            ps = PS.next()
            for kc in range(8):
                kb.mm(ps[0:NST, 0:512], xn[:, kc, si * NST:(si + 1) * NST], wv[:, kc, :], kc == 0, kc == 7)
            vf = FR.next()
            kb.cp(vf[0:NST, :], ps[0:NST, :], eng='act')
            kb.dma('sp', V(o_vs.h[si][:, fg * 512:(fg + 1) * 512], o_vs), vf[0:NST, :])
            vb = mhs.next()
            kb.cp(vb[0:NST, :], vf[0:NST, :])
            kb.dma('sp', V(vns.h[si][:, fg * 512:(fg + 1) * 512], vns), vb[0:NST, :])

    if B_STAGE == 3:
        return early()
    qT = big
    oh = big

    def normalize(psO, ncols, out3, nh, w):
        osb = FR.next()
        kb.cp(osb[0:65, 0:ncols], psO[0:65, 0:ncols], eng='act')
        psl = PSa.next()
        kb.mm(psl[0:64, 0:ncols], l64[0:65, 0:64], osb[0:65, 0:ncols])
        rl = FR.next()
        kb.ts(rl[0:64, 0:ncols], psl[0:64, 0:ncols], 1e-30, None, ALU.add)
        kb.recip(rl[0:64, 0:ncols], rl[0:64, 0:ncols])
        kb.tt(out3, osb[0:64, 0:ncols].re("d (h q) -> d h q", q=w), rl[0:64, 0:ncols].re("d (h q) -> d h q", q=w), ALU.mult)

    def attn_prompt(qb0, nqb):
        for i in range(nqb):
            qb = qb0 + i
            tiles = [(gg, d) for gg in range(3) for d in range(NDEL[gg])]
            for hq in range(4):
                kb.mm(PSall[hq][0:65, 0:512], zerob[:, 0:65], zerob[:, 0:512], True, False)
            for ti, (gg, d) in enumerate(tiles):
                kbk = qb - d
                kt_t = KTS.next(); vt = VSL.next()
                ktv = kt_t[:, 0:1024].re("p (c k) -> p c k", k=128)
                kb.dma('sp', ktv, V(kts.h[:, gg * 8:(gg + 1) * 8, kbk * 128:(kbk + 1) * 128], kts))
                kb.dma('sp', vt[:, :, 0:64], V(vts.h[kbk][:, gg * 1024:(gg + 1) * 1024].rearrange("k (h d) -> k h d", d=64), vts))
                mk = am[:, MIDX0[gg] + d, :]
                for hq in range(4):
                    ps = PSa.next()
                    heads = [2 * (4 * (hq % 2) + e) + hq // 2 for e in range(4)]
                    for e, hs in enumerate(heads):
                        pair, half = hs // 2, hs % 2
                        sl = slice(e * 128, (e + 1) * 128)
                        kb.mm(ps[:, sl], ktv[64 * half:64 * half + 64, pair, :],
                              qT[64 * half:64 * half + 64, gg * 8 + pair, i * 128:(i + 1) * 128], True, True)
                    pt = mhs.next()
                    kb.act(pt.v(), ps.v(), AF.Exp, scale=0.125)
                    for e in range(4):
                        sl = slice(e * 128, (e + 1) * 128)
                        if kbk <= 16:
                            kb.stt(pt[:, sl], pt[:, sl], par[:, 64:65], mk, ALU.mult, ALU.mult)
                        else:
                            kb.tt(pt[:, sl], pt[:, sl], mk, ALU.mult)
                    for e, hs in enumerate(heads):
                        kb.mm(PSall[hq][0:65, e * 128:(e + 1) * 128], vt[:, hs, 0:65], pt[:, e * 128:(e + 1) * 128],
                              False, False)
            for hq in range(4):
                kb.mm(PSall[hq][0:65, 0:512], zerob[:, 0:65], zerob[:, 0:512], False, True)
            for hq in range(4):
                h0 = 24 + 8 * (hq % 2) + hq // 2
                normalize(PSall[hq], 512, oh[0:64, h0:h0 + 7:2, i * 128:(i + 1) * 128], 4, 128)

    def attn_sample():
        for si in range(NSB):
            tiles = []
            for gg in range(3):
                tiles += [(gg, idx) for idx in range(NCACHE[gg])] + [(gg, -1)]
            kb.mm(PSall[0][0:65, 0:128], zerob[:, 0:65], zerob[:, 0:128], True, False)
            for ti, (gg, idx) in enumerate(tiles):
                vt = VSL.next()
                if idx >= 0:
                    kt_t = KTS.next()
                    ktv = kt_t[:, 0:1024].re("p (c k) -> p c k", k=128)
                    kb.dma('pool', ktv, V(ckT[gg].ap[si, :, :, idx * 128:(idx + 1) * 128].rearrange("c p k -> p c k"), None))
                    kb.dma('pool', vt[:, :, 0:64], V(cvv[gg].ap[si, idx * 128:(idx + 1) * 128, :].rearrange("k (h d) -> k h d", d=64), None))
                    nk = 128
                    mk = sm[:, SM0[gg] + idx, :]
                else:
                    ktv = kns[:, gg * 8:(gg + 1) * 8, si * NST:(si + 1) * NST]
                    kb.dma('sp', vt[0:NST, :, 0:64], V(vns.h[si][:, gg * 1024:(gg + 1) * 1024].rearrange("k (h d) -> k h d", d=64), vns))
                    nk = NST
                    mk = smn[:, gg, :]
                pts = []
                for half in range(2):
                    ps = PSa.next()
                    for pair in range(8):
                        sl = slice(pair * NST, (pair + 1) * NST)
                        kb.mm(ps[0:nk, sl], ktv[64 * half:64 * half + 64, pair, 0:nk],
                              qT[64 * half:64 * half + 64, gg * 8 + pair, si * NST:(si + 1) * NST], True, True)
                    pt = mhs.next()
                    kb.act(pt[0:nk, 0:64], ps[0:nk, 0:64], AF.Exp, scale=0.125)
                    kb.tt(pt[0:nk, 0:64], pt[0:nk, 0:64], mk[0:nk, 0:64], ALU.mult)
                    pts.append(pt)
                for hs in range(16):
                    pair, half = hs // 2, hs % 2
                    sl = slice(hs * NST, (hs + 1) * NST)
                    kb.mm(PSall[0][0:65, sl], vt[0:nk, hs, 0:65], pts[half][0:nk, pair * NST:(pair + 1) * NST], False, False)
            kb.mm(PSall[0][0:65, 0:128], zerob[:, 0:65], zerob[:, 0:128], False, True)
            normalize(PSall[0], 128, oh[0:64, 24:40, si * NST:(si + 1) * NST], 16, NST)

    def b_layer(lb, h, NT, segs, sample, qb0):
        li = 2 + lb
        rms_mod(h, NT, 2 * li, segs, None, xn)
        if sample:
            rc, rs = load_rope(0, NT, ropes, 0)
        else:
            rc, rs = load_rope(0, NT, ropek, qb0 * 128)

        def ev_q(j, ps, w):
            rope_evac(ps, NT, rc, rs, qT[:, j, 0:NT])
        linear(wview(w_q, lb), 8, 0, 3072, 512, lambda kc: xn[:, kc, 0:NT], NT, ev_q)
        if sample:
            attn_sample()
        else:
            attn_prompt(qb0, NT // 128)

        def ev_o(m, ps, w):
            for (c0, ln, mc) in segs:
                kb.stt(h[:, m, c0:c0 + ln], ps[:, c0:c0 + ln], GTM[:, 2 * li, m, mc:mc + 1], h[:, m, c0:c0 + ln], ALU.mult, ALU.add)
        linear(wview(w_o, lb, 64), 16, 0, D, 256, lambda kc: oh[0:64, 24 + kc, 0:NT], NT, ev_o, npart=64)
        ffn(li, h, NT, segs, sample)

    yfin = V(yn.h[:].rearrange("p a b -> p (a b)").bitcast(F32), yn)

    def final(h, NT, segs, dst):
        yf = yfin[:, 0:8 * NT].re("p (k t) -> p k t", t=NT)
        rms_mod(h, NT, None, segs, P('final_norm'), yf)
        kb.dma('sp', dst, yf)

    kb.dma('pool', hT[:, :, 0:128].re("p k (b c) -> p k b c", c=128),
           V(h1s.h[bass.ds(off + 16, 1)].rearrange("b p k c -> p k b c"), h1s))
    for lb in range(2):
        b_layer(lb, hT, 128, [(0, 128, 0)], False, 16)
    if B_STAGE == 4:
        return early()
    for li in (2, 3):
        kb.ts(fc4[:, li, :, :], fc4[:, li, :, :], par[:, 64:65], None, ALU.mult)
    for t in range(4):
        kb.dma('pool', hT.v().re("p k (b c) -> p k b c", c=128),
               V(h1s.h[bass.ds(off + 17 + 4 * t, 4)].rearrange("b p k c -> p k b c"), h1s))
        for lb in range(2):
            b_layer(lb, hT, TILE, pseg, False, 17 + 4 * t)
        final(hT, TILE, pseg, V(o_yp.h[:, :, t * TILE:(t + 1) * TILE], o_yp))
    if B_STAGE == 5:
        return early()
    for lb in range(2):
        b_layer(lb, hS, NS, sseg, True, 0)
    final(hS, NS, sseg, o_ys.v())
    kb.dma('sp', o_ffnp.v(), ffnc.v()); kb.dma('sp', o_ffns.v(), ffncs.v())
    kb.finish(outs)
    return nc, es


PK_OFF = {}
PK_N = 0
DO_B = True
B_STAGE = 9
KVS = 9
A_TILES = NTILES


def _pk_layout():
    global PK_N
    o = 0

    def add(key, n):
        nonlocal o
        PK_OFF[key] = (o, n)
        o += n
    for li in range(4):
        add(('ada_b', li), 48); add(('norm_mix', li), 8); add(('norm_ffn', li), 8)
        add(('ffn_conv_w', li), 132); add(('ffn_conv_b', li), 44)
    for li in range(2):
        add(('m_conv_w', li), 96); add(('m_conv_b', li), 24); add(('m_norm', li), 16)
        add(('m_dt_bias', li), 1); add(('A_log', li), 1); add(('m_D', li), 16)
    add('kv_norm', 8); add('kv_ada_b', 16); add('final_norm', 8)
    PK_N = o


_pk_layout()


def fm(v):
    return np.ascontiguousarray(np.asarray(v, np.float32).reshape(-1, 128).T)


def fm2(a):
    a = np.asarray(a, np.float32)
    T_, F_ = a.shape
    return np.ascontiguousarray(a.T.reshape(F_ // 128, 128, T_).transpose(1, 0, 2))


def convw(w):
    K = w.shape[0]
    return np.ascontiguousarray(np.asarray(w, np.float32).T.reshape(-1, 128, K).transpose(1, 0, 2).reshape(128, -1))


def col32(v):
    o = np.zeros((128, 1), np.float32)
    o[:32, 0] = v
    return o


def build_pack(p):
    pk = np.zeros((128, PK_N), np.float32)

    def put(key, arr):
        o, n = PK_OFF[key]
        assert arr.shape == (128, n), (key, arr.shape, n)
        pk[:, o:o + n] = arr
    for li in range(4):
        put(('ada_b', li), fm(p['ada_b'][li])); put(('norm_mix', li), fm(p['norm_mix'][li]))
        put(('norm_ffn', li), fm(p['norm_ffn'][li]))
        put(('ffn_conv_w', li), convw(p['ffn_conv_w'][li])); put(('ffn_conv_b', li), fm(p['ffn_conv_b'][li]))
    for li in range(2):
        put(('m_conv_w', li), convw(p['m_conv_w'][li])); put(('m_conv_b', li), fm(p['m_conv_b'][li]))
        put(('m_norm', li), fm(p['m_norm'][li]))
        put(('m_dt_bias', li), col32(p['m_dt_bias'][li])); put(('A_log', li), col32(p['m_A_log'][li]))
        put(('m_D', li), fm(np.repeat(np.asarray(p['m_D'][li], np.float32), 64)))
    put('kv_norm', fm(p['kv_norm'])); put('kv_ada_b', fm(p['kv_ada_b'])); put('final_norm', fm(p['final_norm']))
    return pk


def rope_tab(pos):
    half = 32
    inv = (np.float32(10000.0) ** (-np.arange(half, dtype=np.float32) * np.float32(2.0 / 64))).astype(np.float32)
    ang = pos.astype(np.float32)[None, :] * inv[:, None]
    c = np.cos(ang).astype(np.float32); s_ = np.sin(ang).astype(np.float32)
    idx = np.arange(128) % 32
    return np.ascontiguousarray(np.stack([c[idx], s_[idx]], axis=1))


def consts():
    c = {}
    c['c_if'] = np.eye(128, dtype=np.float32)
    c['c_ones'] = np.ones((128, 128), np.float32)
    s_ = np.arange(128)
    c['c_tri'] = (s_[:, None] <= s_[None, :]).astype(np.float32)
    sel = np.zeros((32, 32, 128), np.float32)
    for h in range(32):
        sel[h, h, :] = 1.0
    c['c_sel'] = sel.reshape(32, -1)
    ex = np.zeros((32, 16, 128), np.float32)
    for pr in range(16):
        ex[2 * pr, pr, 0:64] = 1.0
        ex[2 * pr + 1, pr, 64:128] = 1.0
    c['c_exp'] = ex.reshape(32, -1)
    rot = np.zeros((128, 128), np.float32)
    for m in range(128):
        if m % 64 < 32:
            rot[m + 32, m] = -1.0
        else:
            rot[m - 32, m] = 1.0
    c['c_rot'] = rot
    l64 = np.zeros((128, 64), np.float32); l64[64, :] = 1.0
    c['c_l64'] = l64
    am = np.zeros((128, 24, 128), np.float32)
    k_ = np.arange(128)[:, None]; q_ = np.arange(128)[None, :]
    for gg, (win, dil) in enumerate(DIL):
        for d in range(NDEL[gg]):
            dq = 128 * d + q_ - k_
            ok = (dq >= 0) & (dq <= win) & (dq % dil == 0)
            am[:, MIDX0[gg] + d, :] = np.where(ok, 1.0, 0.0)
    c['amask'] = am.reshape(128, -1)
    sm = np.zeros((128, 21, 16, NST), np.float32)
    smn = np.zeros((NST, 3, 16, NST), np.float32)
    t_ = np.arange(NST)[None, :]
    for gg, (win, dil) in enumerate(DIL):
        for idx in range(NCACHE[gg]):
            row = (2048 - win) + idx * 128 + np.arange(128)[:, None]
            dist = 2048 + t_ - row
            ok = (dist >= 0) & (dist <= win) & (dist % dil == 0)
            sm[:, SM0[gg] + idx, :, :] = np.where(ok, 1.0, 0.0)[:, None, :]
        dist = t_ - np.arange(NST)[:, None]
        ok = (dist >= 0) & (dist % dil == 0)
        smn[:, gg, :, :] = np.where(ok, 1.0, 0.0)[:, None, :]
    c['smask'] = sm.reshape(128, -1)
    c['smaskn'] = smn.reshape(NST, -1)
    c['ropes'] = rope_tab(8192 + (np.arange(NS) % NST))
    return c


def make_in_maps(p, cores=range(8)):
    cst = consts()
    pk = build_pack(p)
    shared = dict(pk=pk, ada_w=p['ada_w'], kv_ada_w=p['kv_ada_w'], m_w_in=p['m_w_in'], m_w_out=p['m_w_out'],
                  w_kv=p['w_kv'], w_q=p['w_q'], w_o=p['w_o'], ffn_w_up=p['ffn_w_up'], ffn_w_down=p['ffn_w_down'])
    for k in ('c_if', 'c_ones', 'c_tri', 'c_sel', 'c_exp', 'c_rot', 'c_l64', 'amask', 'smask', 'smaskn', 'ropes'):
        shared[k] = cst[k]
    in_maps = []
    for c in cores:
        b, r = c // 4, c % 4
        sbs = slice(4 * c, 4 * c + 4)
        m = dict(shared)
        m['xp'] = fm2(p['x_prompt'][b])
        m['xs'] = fm2(p['x_sample'][sbs].reshape(NS, D))
        m['cs'] = fm2(np.concatenate([p['c_prompt'][b:b + 1], p['c_sample'][sbs]], axis=0))
        par = np.zeros((128, 72), np.float32)
        par[:, 0:32] = (np.arange(32) % 2 == 0); par[:, 32:64] = (np.arange(32) % 2 == 1)
        par[:, 64] = 1.0 if r > 0 else 0.0
        m['c_par'] = par
        m['c_neg'] = np.full((128, 128), 0.0 if r > 0 else NEG, np.float32)
        m['rk'] = np.array([[16 * r, 0, 0, 0]], np.int32)
        m['ropek'] = rope_tab(2048 * r - 2176 + np.arange(NB_LOC * 128))
        m['st_ssm'] = np.ascontiguousarray(p['state_ssm'][:, sbs].transpose(0, 1, 4, 2, 3).reshape(2, NSB, 128, 2048))
        sc = p['state_conv'][:, sbs]
        m['st_conv'] = np.ascontiguousarray(sc.reshape(2, NSB, 3, 24, 128).transpose(4, 0, 1, 3, 2).reshape(128, -1))
        sf = p['state_ffn_conv'][:, sbs]
        m['st_ffn'] = np.ascontiguousarray(sf.reshape(4, NSB, 2, 44, 128).transpose(4, 0, 1, 3, 2).reshape(128, -1))
        for gg, (win, dil) in enumerate(DIL):
            ck = p['cache_k'][sbs, 2048 - win:, 16 * gg:16 * gg + 16, :]
            m['ck%d' % gg] = np.ascontiguousarray(ck.reshape(NSB, win, 8, 128).transpose(0, 2, 3, 1))
            cv = p['cache_v'][sbs, 2048 - win:, 16 * gg:16 * gg + 16, :]
            m['cv%d' % gg] = np.ascontiguousarray(cv.reshape(NSB, win, 1024))
        in_maps.append(m)
    return in_maps


def kernel(**inp):
    p = {k: np.asarray(v) for k, v in inp.items()}
    nc, es = build_program()
    in_maps = make_in_maps(p)
    res = run_bass_kernel_spmd(nc, in_maps, core_ids=list(range(8)))
    es.close()
    R = res.results
    return assemble(R)


def unfm2(a):
    return np.ascontiguousarray(a.transpose(2, 1, 0).reshape(a.shape[2], -1))


def assemble(R):
    f32 = np.float32
    y_p = np.zeros((2, SEQ, D), f32); y_s = np.zeros((32, NST, D), f32)
    ssm_p = np.zeros((2, 2, 32, 64, 128), f32); ssm_s = np.zeros((2, 32, 32, 64, 128), f32)
    conv_p = np.zeros((2, 2, 3, 3072), f32); conv_s = np.zeros((2, 32, 3, 3072), f32)
    ffn_p = np.zeros((4, 2, 2, 5632), f32); ffn_s = np.zeros((4, 32, 2, 5632), f32)
    k_p = np.zeros((2, 2048, 48, 64), f32); k_s = np.zeros((32, NST, 48, 64), f32)
    v_p = np.zeros((2, 2048, 48, 64), f32); v_s = np.zeros((32, NST, 48, 64), f32)
    for c in range(8):
        b, r = c // 4, c % 4
        o = R[c]
        sbs = slice(4 * c, 4 * c + 4)
        y_p[b, SPAN * r:SPAN * (r + 1)] = unfm2(o['o_yp'])
        y_s[sbs] = unfm2(o['o_ys']).reshape(NSB, NST, D)
        ssm_s[:, sbs] = o['o_ssms'].reshape(2, NSB, 128, 32, 64).transpose(0, 1, 3, 4, 2)
        conv_s[:, sbs] = o['o_convs'].reshape(128, 2, NSB, 24, 3).transpose(1, 2, 4, 3, 0).reshape(2, NSB, 3, 3072)
        ffn_s[:, sbs] = o['o_ffns'].reshape(128, 4, NSB, 44, 2).transpose(1, 2, 4, 3, 0).reshape(4, NSB, 2, 5632)
        k_s[sbs] = unfm2(o['o_ks']).reshape(NSB, NST, 48, 64)
        v_s[sbs] = o['o_vs'].reshape(NSB, NST, 48, 64)
        if r == 3:
            ssm_p[:, b] = o['o_ssmp'].reshape(2, 128, 32, 64).transpose(0, 2, 3, 1)
            conv_p[:, b] = o['o_convp'].reshape(128, 2, 24, 3).transpose(1, 3, 2, 0).reshape(2, 3, 3072)
            ffn_p[:, b] = o['o_ffnp'].reshape(128, 4, 44, 2).transpose(1, 3, 2, 0).reshape(4, 2, 5632)
            k_p[b] = unfm2(o['o_kp']).reshape(2048, 48, 64)
            v_p[b] = o['o_vp'].reshape(2048, 48, 64)
    return (y_p, y_s, ssm_p, ssm_s, conv_p, conv_s, ffn_p, ffn_s, k_p, k_s, v_p, v_s)
```

```python
import numpy as np
import concourse.bass as bass
import concourse.mybir as mybir
from concourse.bass_utils import run_bass_kernel_spmd

F32 = mybir.dt.float32
BF16 = mybir.dt.bfloat16
I32 = mybir.dt.int32
ALU = mybir.AluOpType
AF = mybir.ActivationFunctionType

D = 1024
SEQ = 8192
TILE = 512
NTILES = SEQ // TILE
NSB = 4
NST = 8
NS = NSB * NST
EPS = 1e-6
NEG = -30000.0
SPAN = 2048
DIL = ((128, 1), (512, 4), (2048, 16))
PADC = 2176


class V:
    def __init__(s, ap, t):
        s.ap = ap
        s.t = t

    def __getitem__(s, k):
        return V(s.ap[k], s.t)

    def re(self, pat, **kw):
        return V(self.ap.rearrange(pat, **kw), self.t)

    def bc(s, axis, shape):
        return V(s.ap.unsqueeze(axis).to_broadcast(shape), s.t)

    def bitcast(s, dt):
        return V(s.ap.bitcast(dt), s.t)


class T:
    def __init__(s, h, dram=False):
        s.h = h
        s.w = None
        s.r = {}
        s.dsem = {}
        s.dcnt = {}
        s.dram = dram
        s.multi = dram
        s.ring_i = 0
        s.wl = {}
        s.psum = False

    def __getitem__(s, k):
        return V(s.h[k], s)

    def v(s):
        return V(s.h[:] if not s.dram else s.h, s)


class Ring:
    def __init__(s, items):
        s.items = items
        s.i = 0

    def next(s):
        x = s.items[s.i % len(s.items)]
        s.i += 1
        return x


class KB:
    def __init__(s, nc, es):
        s.nc = nc
        s.es = es
        s.E = {'pe': nc.tensor, 'act': nc.scalar, 'dve': nc.vector, 'pool': nc.gpsimd, 'sp': nc.sync}
        s.esem = {k: es.enter_context(nc.semaphore("e_" + k)) for k in ('pe', 'act', 'dve', 'pool')}
        s.cnt = {k: 0 for k in s.esem}
        s.waited = {k: {} for k in s.E}
        s.semobj = {}
        s.allsem = {}
        s.nsb = 0

    def sb(s, shape, dt, name=None):
        s.nsb += 1
        h = s.es.enter_context(s.nc.sbuf_tensor(name or ("t%d" % s.nsb), list(shape), dt))
        return T(h)

    def dr(s, name, shape, dt, kind="Internal"):
        h = s.nc.dram_tensor(name, list(shape), dt, kind=kind).ap()
        return T(h, dram=True)

    def _waits(s, eng, reads, writes):
        need = {}

        def add(st):
            if st is None:
                return
            sem, val = st
            if need.get(sem.name, (None, 0))[1] < val:
                need[sem.name] = (sem, val)
        for t in reads:
            if t is not None:
                add(t.w)
                for st in t.wl.values():
                    add(st)
                if t.psum:
                    for st in t.r.values():
                        if eng not in s.esem or st[0] is not s.esem[eng]:
                            add(st)
        for t in writes:
            if t is not None:
                add(t.w)
                for st in t.wl.values():
                    add(st)
                for st in t.r.values():
                    add(st)
        for nm, (sem, val) in need.items():
            if eng == 'pe' and sem is s.esem['pe']:
                continue
            if s.waited[eng].get(nm, 0) >= val:
                continue
            s.E[eng].wait_ge(sem, val)
            s.waited[eng][nm] = val

    def _stamp(s, st, reads, writes):
        for t in writes:
            if t is not None:
                t.w = st
                t.r = {}
        for t in reads:
            if t is not None and t not in writes:
                t.r[st[0].name] = st

    def emit(s, eng, fn, reads, writes):
        reads = [v.t for v in reads if isinstance(v, V)]
        writes = [v.t for v in writes if isinstance(v, V)]
        s._waits(eng, reads, writes)
        inst = fn(s.E[eng])
        s.cnt[eng] += 1
        inst.then_inc(s.esem[eng], 1)
        s._stamp((s.esem[eng], s.cnt[eng]), reads, writes)

    def dma(s, q, out, in_, sem_t=None):
        owner = out.t
        reads = [in_.t] if in_.t is not None else []
        writes = [out.t] if out.t is not None else []
        if owner.multi:
            s._waits(q, reads, [])
            key = (q, owner.ring_i % 4)
            owner.ring_i += 1
        else:
            s._waits(q, reads, writes)
            key = (q, 0)
        if key not in owner.dsem:
            owner.dsem[key] = s.es.enter_context(s.nc.semaphore("d%d" % len(s.semobj)))
            s.semobj[owner.dsem[key].name] = owner.dsem[key]
            owner.dcnt[key] = 0
        sem = owner.dsem[key]
        if owner.multi and owner.dcnt[key] > 0 and s.waited[q].get(sem.name, 0) < owner.dcnt[key]:
            s.E[q].wait_ge(sem, owner.dcnt[key])
            s.waited[q][sem.name] = owner.dcnt[key]
        s.E[q].dma_start(out=out.ap, in_=in_.ap).then_inc(sem, 16)
        owner.dcnt[key] += 16
        st = (sem, owner.dcnt[key])
        s.allsem[sem.name] = st
        if owner.multi:
            owner.wl[sem.name] = st
            for t in reads:
                t.r[sem.name] = st
        else:
            s._stamp(st, reads, writes)

    def mm(s, out, l, r, start=True, stop=True):
        s.emit('pe', lambda e: e.matmul(out.ap, lhsT=l.ap, rhs=r.ap, start=start, stop=stop), [l, r], [out])

    def tr(s, out, in_, ident):
        s.emit('pe', lambda e: e.transpose(out.ap, in_.ap, ident.ap), [in_, ident], [out])

    def act(s, out, in_, func, bias=None, scale=1.0):
        rd = [in_] + ([bias] if isinstance(bias, V) else [])
        kw = {}
        if bias is not None:
            kw['bias'] = bias.ap if isinstance(bias, V) else bias
        sc = scale.ap if isinstance(scale, V) else scale
        if isinstance(scale, V):
            rd.append(scale)
        s.emit('act', lambda e: e.activation(out=out.ap, in_=in_.ap, func=func, scale=sc, **kw), rd, [out])

    def tt(s, out, a, b, op, eng='dve'):
        s.emit(eng, lambda e: e.tensor_tensor(out=out.ap, in0=a.ap, in1=b.ap, op=op), [a, b], [out])

    def ts(s, out, a, s1, s2, op0, op1=None, eng='dve'):
        rd = [a] + [x for x in (s1, s2) if isinstance(x, V)]
        a1 = s1.ap if isinstance(s1, V) else s1
        a2 = s2.ap if isinstance(s2, V) else s2
        if op1 is None:
            s.emit(eng, lambda e: e.tensor_scalar(out=out.ap, in0=a.ap, scalar1=a1, scalar2=None, op0=op0), rd, [out])
        else:
            s.emit(eng, lambda e: e.tensor_scalar(out=out.ap, in0=a.ap, scalar1=a1, scalar2=a2, op0=op0, op1=op1), rd, [out])

    def stt(s, out, a, sc, b, op0, op1, eng='dve'):
        rd = [a, b] + ([sc] if isinstance(sc, V) else [])
        a1 = sc.ap if isinstance(sc, V) else sc
        s.emit(eng, lambda e: e.scalar_tensor_tensor(out=out.ap, in0=a.ap, scalar=a1, in1=b.ap, op0=op0, op1=op1), rd, [out])

    def cp(s, out, a, eng='dve'):
        if eng == 'act':
            s.act(out, a, AF.Copy)
        else:
            s.emit(eng, lambda e: e.tensor_copy(out=out.ap, in_=a.ap), [a], [out])

    def memset(s, out, val, eng='dve'):
        s.emit(eng, lambda e: e.memset(out.ap, val), [], [out])

    def recip(s, out, a):
        s.emit('dve', lambda e: e.reciprocal(out=out.ap, in_=a.ap), [a], [out])

    def finish(s, tiles):
        for (sem, val) in s.allsem.values():
            s.E['sp'].wait_ge(sem, val)
        for k in ('pe', 'act', 'dve'):
            if s.cnt[k] > 0:
                s.E['sp'].wait_ge(s.esem[k], s.cnt[k])


NB_LOC = 33
NDEL = (2, 5, 17)
MIDX0 = (0, 2, 7)
NCACHE = (1, 4, 16)
SM0 = (0, 1, 5)


def build_program():
    from contextlib import ExitStack
    nc = bass.Bass("TRN2", target_bir_lowering=False)
    es = ExitStack()
    kb = KB(nc, es)

    def din(name, shape, dt=F32):
        return V(nc.dram_tensor(name, list(shape), dt, kind="ExternalInput").ap(), None)

    def dout(name, shape, dt=F32):
        return kb.dr(name, shape, dt, kind="ExternalOutput")

    xp = din("xp", [128, 8, SEQ]); xs = din("xs", [128, 8, NS]); csin = din("cs", [128, 8, 5])
    pk = din("pk", [128, PK_N])
    c_if = din("c_if", [128, 128]); c_ones = din("c_ones", [128, 128]); c_tri = din("c_tri", [128, 128])
    c_sel = din("c_sel", [32, 32 * 128]); c_exp = din("c_exp", [32, 16 * 128]); c_rot = din("c_rot", [128, 128])
    c_par = din("c_par", [128, 72]); c_l64 = din("c_l64", [128, 64]); c_neg = din("c_neg", [128, 128])
    rk = din("rk", [1, 4], I32)
    ropek = din("ropek", [128, 2, NB_LOC * 128]); ropes = din("ropes", [128, 2, NS])
    amask = din("amask", [128, 24 * 128]); smask = din("smask", [128, 21 * 128]); smaskn = din("smaskn", [NST, 3 * 128])
    st_ssm = din("st_ssm", [2, NSB, 128, 2048])
    st_conv = din("st_conv", [128, 2 * NSB * 24 * 3]); st_ffn = din("st_ffn", [128, 4 * NSB * 44 * 2])
    ckT = [din("ck%d" % g, [NSB, 8, 128, DIL[g][0]]) for g in range(3)]
    cvv = [din("cv%d" % g, [NSB, DIL[g][0], 1024]) for g in range(3)]
    ada_w = din("ada_w", [4, D, 6 * D]); kv_ada_w = din("kv_ada_w", [D, 2 * D])
    m_w_in = din("m_w_in", [2, D, 5152]); m_w_out = din("m_w_out", [2, 2048, D])
    w_kv = din("w_kv", [D, 6144]); w_q = din("w_q", [2, D, 3072]); w_o = din("w_o", [2, D, D])
    f_up = din("ffn_w_up", [4, D, 5632]); f_dn = din("ffn_w_down", [4, 2816, D])
    o_yp = dout("o_yp", [128, 8, SPAN]); o_ys = dout("o_ys", [128, 8, NS])
    o_ssmp = dout("o_ssmp", [2, 128, 2048]); o_ssms = dout("o_ssms", [2, NSB, 128, 2048])
    o_convp = dout("o_convp", [128, 2 * 24 * 3]); o_convs = dout("o_convs", [128, 2 * NSB * 24 * 3])
    o_ffnp = dout("o_ffnp", [128, 4 * 44 * 2]); o_ffns = dout("o_ffns", [128, 4 * NSB * 44 * 2])
    o_kp = dout("o_kp", [128, 24, SPAN]); o_ks = dout("o_ks", [128, 24, NS])
    o_vp = dout("o_vp", [16, 128, 3072]); o_vs = dout("o_vs", [NSB, NST, 3072])
    outs = [o_yp, o_ys, o_ssmp, o_ssms, o_convp, o_convs, o_ffnp, o_ffns, o_kp, o_ks, o_vp, o_vs]
    h1s = kb.dr("h1s", [(PADC + SEQ) // 128, 128, 8, 128], F32)
    kts = kb.dr("kts", [128, NB_LOC, 24, 128], BF16)
    vts = kb.dr("vts", [NB_LOC, 128, 3072], BF16)
    vns = kb.dr("vns", [NSB, NST, 3072], BF16)

    sb = kb.sb
    identf = sb([128, 128], F32); onesf = sb([128, 128], F32); trif = sb([128, 128], F32)
    identb = sb([128, 128], BF16); trib = sb([128, 128], BF16); rotb = sb([128, 128], BF16); negb = sb([128, 128], BF16)
    selb = sb([32, 32 * 128], BF16); expb = sb([32, 16 * 128], BF16)
    par = sb([128, 72], F32); l64 = sb([128, 64], F32)
    pkt = sb([128, PK_N], F32)
    cst = sb([128, 8, 5], F32); csb = sb([128, 8, 5], BF16)
    mods = sb([128, 48, 5], F32)
    GM = sb([128, 9, 8, 5], F32); SHM = sb([128, 9, 8, 5], F32); GTM = sb([128, 8, 8, 5], F32)
    zerob = sb([128, 512], BF16)
    for t, src in [(identf, c_if), (onesf, c_ones), (trif, c_tri), (par, c_par), (pkt, pk), (cst, csin), (l64, c_l64)]:
        kb.dma('sp', t.v(), src)
    for t, src in [(identb, c_if), (trib, c_tri), (rotb, c_rot), (selb, c_sel), (expb, c_exp), (negb, c_neg)]:
        kb.dma('pool', t.v(), src)
    kb.memset(zerob.v(), 0.0)
    kb.act(csb.v(), cst.v(), AF.Silu)

    def P(name, li=None):
        o, n = PK_OFF[name if li is None else (name, li)]
        return pkt[:, o:o + n]

    PSall = [T(es.enter_context(nc.psum_tensor("ps%d" % i, [128, 512], F32))) for i in range(8)]
    for t_ in PSall:
        t_.psum = True
    PS = Ring(PSall)
    WB = Ring([sb([128, 4096], BF16) for _ in range(2)])

    def wload(wd, kcn, f0, fw, npart=128):
        wb = WB.next()
        view = V(wb.h[0:npart, 0:kcn * fw].rearrange("p (k f) -> p k f", f=fw), wb)
        kb.dma('pool', view, V(wd.ap[:, :, f0:f0 + fw], None))
        return view

    def linear(wd, kcn, F0, F, gw, rhs_fn, NT, evac, npart=128):
        ng = -(-F // gw)
        nxt = wload(wd, kcn, F0, min(gw, F), npart)
        for g in range(ng):
            wv = nxt
            if g + 1 < ng:
                nxt = wload(wd, kcn, F0 + (g + 1) * gw, min(gw, F - (g + 1) * gw), npart)
            fw = min(gw, F - g * gw)
            for jj in range(-(-fw // 128)):
                w = min(128, fw - jj * 128)
                ps = PS.next()
                for kc in range(kcn):
                    kb.mm(ps[0:w, 0:NT], wv[:, kc, jj * 128:jj * 128 + w], rhs_fn(kc), kc == 0, kc == kcn - 1)
                evac(g * (gw // 128) + jj, ps, w)

    def wview(w3, li, p=128):
        a = w3.ap if li is None else w3.ap[li]
        return V(a.rearrange("(k p) f -> p k f", p=p), None)

    for li in range(4):
        def ev(j, ps, w, li=li):
            kb.ts(mods[:, j, :], ps[:, 0:5], P('ada_b', li)[:, j:j + 1], None, ALU.add)
        linear(wview(ada_w, li), 8, 0, 6 * D, 512, lambda kc: csb[:, kc, :], 5, ev)
        for sub in range(2):
            k = 2 * li + sub
            nrm = P('norm_mix' if sub == 0 else 'norm_ffn', li)
            b0 = 24 * sub
            for kc in range(8):
                kb.ts(GM[:, k, kc, :], mods[:, b0 + 8 + kc, :], 1.0, nrm[:, kc:kc + 1], ALU.add, ALU.mult)
            kb.cp(SHM[:, k, :, :], mods[:, b0:b0 + 8, :])
            kb.cp(GTM[:, k, :, :], mods[:, b0 + 16:b0 + 24, :])

    def evkv(j, ps, w):
        kb.ts(mods[:, j, :], ps[:, 0:5], P('kv_ada_b')[:, j:j + 1], None, ALU.add)
    linear(wview(kv_ada_w, None), 8, 0, 2 * D, 512, lambda kc: csb[:, kc, :], 5, evkv)
    for kc in range(8):
        kb.ts(GM[:, 8, kc, :], mods[:, 8 + kc, :], 1.0, P('kv_norm')[:, kc:kc + 1], ALU.add, ALU.mult)
    kb.cp(SHM[:, 8, :, :], mods[:, 0:8, :])

    hT = sb([128, 8, TILE], F32); hS = sb([128, 8, NS], F32)
    xn = sb([128, 8, TILE], BF16)
    big = sb([128, 40, TILE], BF16)
    yn = sb([128, 16, TILE], BF16)
    FR = Ring([sb([128, TILE], F32) for _ in range(5)])
    rtile = sb([128, TILE], F32)
    prec = Ring([sb([128, NSB * 11 + TILE], F32) for _ in range(3)])
    ropet = sb([128, 2, TILE], F32)
    negA = sb([32, 2], F32)
    HTp = sb([128, 2, 2048], F32)
    HTb = sb([128, 2048], BF16)
    convc = sb([128, 2 * 24 * 3], F32); ffnc = sb([128, 4 * 44 * 2], F32)
    convcs = sb([128, 2 * NSB * 24 * 3], F32); ffncs = sb([128, 4 * NSB * 44 * 2], F32)
    adt = sb([128, 128], F32); acs = sb([128, 64 + 128], F32); w2 = sb([128, 96], F32)
    achl = sb([32, 2, 128], BF16)
    xTt = Ring([sb([128, 512], BF16) for _ in range(2)])
    xdt0 = sb([128, 2048], BF16); xdt1 = sb([128, 2048], BF16); xdd = sb([128, 2048], BF16)
    btm = sb([128, 512], BF16); cbm = sb([128, 512], BF16)
    mhs = Ring([sb([128, 512], BF16) for _ in range(3)])
    vsl = [sb([128, 16, 65], BF16) for _ in range(3)]
    kns = sb([128, 24, NS], BF16)
    dtT = ropet[0:32, 0, :]; aT = ropet[0:32, 1, :]
    cc4 = convc.v().re("p (l j k) -> p l j k", l=2, j=24)
    fc4 = ffnc.v().re("p (l j k) -> p l j k", l=4, j=44)
    ccs5 = convcs.v().re("p (l s j k) -> p l s j k", l=2, s=NSB, j=24)
    fcs5 = ffncs.v().re("p (l s j k) -> p l s j k", l=4, s=NSB, j=44)
    kb.dma('sp', convcs.v(), st_conv); kb.dma('sp', ffncs.v(), st_ffn)
    kb.memset(convc.v(), 0.0); kb.memset(ffnc.v(), 0.0)
    kb.memset(HTp.v(), 0.0)
    for li in range(2):
        kb.act(negA[:, li:li + 1], P('A_log', li)[0:32, 0:1], AF.Exp)
        kb.ts(negA[:, li:li + 1], negA[:, li:li + 1], -1.0, None, ALU.mult)
    zs = big; xbc = big; gbuf = big

    def rms_mod(h, NT, k, segs, gcol, out):
        ps = PS.next()
        for kc in range(8):
            sq = FR.next()
            kb.act(sq[:, 0:NT], h[:, kc, 0:NT], AF.Square)
            kb.mm(ps[:, 0:NT], onesf.v(), sq[:, 0:NT], kc == 0, kc == 7)
        r = rtile
        kb.ts(r[:, 0:NT], ps[:, 0:NT], 1.0 / D, EPS, ALU.mult, ALU.add)
        kb.act(r[:, 0:NT], r[:, 0:NT], AF.Sqrt)
        kb.recip(r[:, 0:NT], r[:, 0:NT])
        for kc in range(8):
            t = FR.next()
            kb.tt(t[:, 0:NT], h[:, kc, 0:NT], r[:, 0:NT], ALU.mult)
            for (c0, ln, mc) in segs:
                if gcol is None:
                    kb.ts(out[:, kc, c0:c0 + ln], t[:, c0:c0 + ln], GM[:, k, kc, mc:mc + 1], SHM[:, k, kc, mc:mc + 1], ALU.mult, ALU.add)
                else:
                    kb.ts(out[:, kc, c0:c0 + ln], t[:, c0:c0 + ln], gcol[:, kc:kc + 1], None, ALU.mult)

    def conv_chunk(ps, NT, segs, K, wcol, bcol, carry_fn, func, out_v):
        pc = prec.next()
        L = segs[0][1]
        nseg = len(segs)
        pv = pc[:, 0:nseg * (K - 1 + L)].re("p (s l) -> p s l", l=K - 1 + L)
        for si in range(nseg):
            kb.cp(pv[:, si, 0:K - 1], carry_fn(si))
        kb.cp(pv[:, :, K - 1:K - 1 + L], ps[:, 0:NT].re("p (s l) -> p s l", l=L), eng='act')
        for si in range(nseg):
            kb.cp(carry_fn(si), pv[:, si, L:L + K - 1])
        ac = FR.next()
        av = ac[:, 0:NT].re("p (s l) -> p s l", l=L)
        kb.act(av, pv[:, :, 0:L], AF.Copy, scale=wcol(0))
        for k in range(1, K):
            kb.stt(av, pv[:, :, k:k + L], wcol(k), av, ALU.mult, ALU.add)
        kb.act(out_v, ac[:, 0:NT], func, bias=bcol)

    def ssd_chunk(li, c0, Q, HT):
        ps = PS.next()
        kb.mm(ps[0:Q, 0:32], aT[0:32, c0:c0 + Q], identf[0:32, 0:32])
        kb.mm(ps[0:Q, 32:64], dtT[0:32, c0:c0 + Q], identf[0:32, 0:32])
        kb.cp(adt[0:Q, 0:64], ps[0:Q, 0:64])
        kb.tt(adt[0:Q, 64:96], adt[0:Q, 32:64], par[0:Q, 0:32], ALU.mult)
        kb.tt(adt[0:Q, 96:128], adt[0:Q, 32:64], par[0:Q, 32:64], ALU.mult)
        ps2 = PS.next()
        kb.mm(ps2[0:Q, 0:32], trif[0:Q, 0:Q], adt[0:Q, 0:32])
        kb.mm(ps2[:, 32:64], onesf[0:Q, :], adt[0:Q, 0:32])
        kb.mm(ps2[0:32, 64:64 + Q], adt[0:Q, 0:32], trif[0:Q, 0:Q])
        kb.cp(acs[0:Q, 0:32], ps2[0:Q, 0:32])
        kb.cp(acs[:, 32:64], ps2[:, 32:64])
        kb.cp(acs[0:32, 64:64 + Q], ps2[0:32, 64:64 + Q])
        kb.cp(achl[:, 0, 0:Q], acs[0:32, 64:64 + Q])
        kb.tt(achl[:, 1, 0:Q], acs[0:32, 64:64 + Q], achl[:, 0, 0:Q], ALU.subtract)
        kb.tt(w2[0:Q, 0:32], acs[0:Q, 32:64], acs[0:Q, 0:32], ALU.subtract)
        kb.act(w2[0:Q, 0:32], w2[0:Q, 0:32], AF.Exp)
        kb.tt(w2[0:Q, 0:32], w2[0:Q, 0:32], adt[0:Q, 32:64], ALU.mult)
        kb.act(w2[:, 32:64], acs[:, 32:64], AF.Exp)
        for grp in range(4):
            psb = PS.next()
            pb = psb.v().bitcast(BF16)
            for j4 in range(4):
                kb.tr(pb[0:Q, j4 * 128:(j4 + 1) * 128], xbc[:, 16 + grp * 4 + j4, c0:c0 + Q], identb.v())
            xT = xTt.next()
            kb.cp(xT[0:Q, :], pb[0:Q, 0:512], eng='act')
            x3 = xT[0:Q, :].re("q (h p) -> q h p", p=64)
            sl = slice(grp * 512, (grp + 1) * 512)
            kb.tt(xdt0[0:Q, sl].re("q (h p) -> q h p", p=64), x3, adt[0:Q, 64 + grp * 8:72 + grp * 8].bc(2, [Q, 8, 64]), ALU.mult)
            kb.tt(xdt1[0:Q, sl].re("q (h p) -> q h p", p=64), x3, adt[0:Q, 96 + grp * 8:104 + grp * 8].bc(2, [Q, 8, 64]), ALU.mult)
            kb.tt(xdd[0:Q, sl].re("q (h p) -> q h p", p=64), x3, w2[0:Q, grp * 8:grp * 8 + 8].bc(2, [Q, 8, 64]), ALU.mult)
        psb = PS.next()
        pb = psb.v().bitcast(BF16)
        for g in range(4):
            kb.tr(pb[0:Q, g * 128:(g + 1) * 128], xbc[:, 32 + g, c0:c0 + Q], identb.v())
        kb.cp(btm[0:Q, :], pb[0:Q, 0:512], eng='act')
        ps = PS.next()
        for g in range(4):
            kb.mm(ps[0:Q, g * Q:(g + 1) * Q], xbc[:, 32 + g, c0:c0 + Q], xbc[:, 36 + g, c0:c0 + Q])
        kb.tt(cbm[0:Q, 0:4 * Q].re("q (g l) -> q g l", l=Q), ps[0:Q, 0:4 * Q].re("q (g l) -> q g l", l=Q),
              trib[0:Q, 0:Q].bc(1, [Q, 4, Q]), ALU.mult)
        for bt in range(4):
            mh = {}
            for hb in (2 * bt, 2 * bt + 1):
                ps = PS.next()
                for e in range(4):
                    h = 4 * hb + e
                    kb.mm(ps[0:Q, e * Q:(e + 1) * Q], selb[0:32, h * 128:h * 128 + Q], achl[:, 0, 0:Q], True, False)
                    kb.mm(ps[0:Q, e * Q:(e + 1) * Q], selb[0:32, h * 128:h * 128 + Q], achl[:, 1, 0:Q], False, True)
                d = FR.next()
                for e in range(4):
                    h = 4 * hb + e
                    kb.ts(d[0:Q, e * Q:(e + 1) * Q], ps[0:Q, e * Q:(e + 1) * Q], acs[0:Q, h:h + 1], 0.0, ALU.subtract, ALU.min)
                m = mhs.next()
                kb.act(m[0:Q, 0:4 * Q], d[0:Q, 0:4 * Q], AF.Exp)
                kb.tt(m[0:Q, 0:4 * Q].re("q (e l) -> q e l", l=Q), m[0:Q, 0:4 * Q].re("q (e l) -> q e l", l=Q),
                      cbm[0:Q, bt * Q:(bt + 1) * Q].bc(1, [Q, 4, Q]), ALU.mult)
                mh[hb] = m
            psY = PS.next(); psO = PS.next(); psE = PS.next()
            for pr in range(4):
                pair = 4 * bt + pr
                for e in range(2):
                    h = 2 * pair + e
                    xd = xdt0 if e == 0 else xdt1
                    kb.mm(psY[:, pr * Q:(pr + 1) * Q], xd[0:Q, pair * 128:(pair + 1) * 128],
                          mh[h // 4][0:Q, (h % 4) * Q:(h % 4 + 1) * Q], e == 0, e == 1)
                kb.mm(psO[:, pr * Q:(pr + 1) * Q], HTb[:, pair * 128:(pair + 1) * 128], xbc[:, 36 + bt, c0:c0 + Q])
                kb.mm(psE[:, pr * Q:(pr + 1) * Q], expb[0:32, pair * 128:(pair + 1) * 128], achl[:, 0, 0:Q], True, False)
                kb.mm(psE[:, pr * Q:(pr + 1) * Q], expb[0:32, pair * 128:(pair + 1) * 128], achl[:, 1, 0:Q], False, True)
            et = FR.next(); yt = FR.next()
            kb.act(et[:, 0:4 * Q], psE[:, 0:4 * Q], AF.Exp)
            kb.tt(et[:, 0:4 * Q], psO[:, 0:4 * Q], et[:, 0:4 * Q], ALU.mult)
            kb.tt(yt[:, 0:4 * Q], psY[:, 0:4 * Q], et[:, 0:4 * Q], ALU.add)
            for pr in range(4):
                pair = 4 * bt + pr
                kb.stt(yt[:, pr * Q:(pr + 1) * Q], xbc[:, 16 + pair, c0:c0 + Q], P('m_D', li)[:, pair:pair + 1],
                       yt[:, pr * Q:(pr + 1) * Q], ALU.mult, ALU.add)
            kb.tt(yt[:, 0:4 * Q].re("p (c l) -> p c l", l=Q), yt[:, 0:4 * Q].re("p (c l) -> p c l", l=Q),
                  zs[:, 4 * bt:4 * bt + 4, c0:c0 + Q], ALU.mult)
            sq = FR.next()
            kb.act(sq[:, 0:4 * Q], yt[:, 0:4 * Q], AF.Square)
            psN = PS.next()
            for pr in range(4):
                kb.mm(psN[:, 0:Q], onesf.v(), sq[:, pr * Q:(pr + 1) * Q], pr == 0, pr == 3)
            r = sq
            kb.ts(r[:, 0:Q], psN[:, 0:Q], 1.0 / 512, EPS, ALU.mult, ALU.add)
            kb.act(r[:, 0:Q], r[:, 0:Q], AF.Sqrt)
            kb.recip(r[:, 0:Q], r[:, 0:Q])
            for pr in range(4):
                pair = 4 * bt + pr
                kb.stt(yn[:, pair, c0:c0 + Q], yt[:, pr * Q:(pr + 1) * Q], P('m_norm', li)[:, pair:pair + 1], r[:, 0:Q],
                       ALU.mult, ALU.mult)
        for g in range(4):
            psS = PS.next()
            kb.mm(psS[:, 0:512], btm[0:Q, g * 128:(g + 1) * 128], xdd[0:Q, g * 512:(g + 1) * 512])
            hv = HT[:, g * 512:(g + 1) * 512]
            kb.tt(hv.re("n (h p) -> n h p", p=64), hv.re("n (h p) -> n h p", p=64),
                  w2[:, 32 + g * 8:40 + g * 8].bc(2, [128, 8, 64]), ALU.mult)
            kb.tt(hv, hv, psS[:, 0:512], ALU.add)
        kb.cp(HTb.v(), HT, eng='act')

    def ffn(li, h, NT, segs, sample):
        rms_mod(h, NT, 2 * li + 1, segs, None, xn)
        wc = P('ffn_conv_w', li)

        def ev_up(j, ps, w):
            cf = (lambda si: fcs5[:, li, si, j, :]) if sample else (lambda si: fc4[:, li, j, :])
            if j < 22:
                conv_chunk(ps, NT, segs, 3, lambda k: wc[:, j * 3 + k:j * 3 + k + 1], P('ffn_conv_b', li)[:, j:j + 1],
                           cf, AF.Silu, gbuf[:, j, 0:NT])
            else:
                t = FR.next()
                conv_chunk(ps, NT, segs, 3, lambda k: wc[:, j * 3 + k:j * 3 + k + 1], P('ffn_conv_b', li)[:, j:j + 1],
                           cf, AF.Identity, t[:, 0:NT])
                kb.tt(gbuf[:, j - 22, 0:NT], gbuf[:, j - 22, 0:NT], t[:, 0:NT], ALU.mult)
        linear(wview(f_up, li), 8, 0, 5632, 512, lambda kc: xn[:, kc, 0:NT], NT, ev_up)

        def ev_dn(m, ps, w):
            for (c0, ln, mc) in segs:
                kb.stt(h[:, m, c0:c0 + ln], ps[:, c0:c0 + ln], GTM[:, 2 * li + 1, m, mc:mc + 1], h[:, m, c0:c0 + ln], ALU.mult, ALU.add)
        linear(wview(f_dn, li), 22, 0, D, 128, lambda kc: gbuf[:, kc, 0:NT], NT, ev_dn)

    def a_layer(li, h, NT, segs, sample):
        rms_mod(h, NT, 2 * li, segs, None, xn)
        wc = P('m_conv_w', li)

        def ev_in(j, ps, w):
            if j < 16:
                kb.act(zs[:, j, 0:NT], ps[:, 0:NT], AF.Silu)
            elif j < 40:
                jc = j - 16
                cf = (lambda si: ccs5[:, li, si, jc, :]) if sample else (lambda si: cc4[:, li, jc, :])
                conv_chunk(ps, NT, segs, 4, lambda k: wc[:, jc * 4 + k:jc * 4 + k + 1], P('m_conv_b', li)[:, jc:jc + 1],
                           cf, AF.Silu, xbc[:, j, 0:NT])
            else:
                x = FR.next(); ax = FR.next(); mx = FR.next()
                kb.ts(x[0:32, 0:NT], ps[0:32, 0:NT], P('m_dt_bias', li)[0:32, 0:1], None, ALU.add)
                kb.act(ax[0:32, 0:NT], x[0:32, 0:NT], AF.Abs)
                kb.act(ax[0:32, 0:NT], ax[0:32, 0:NT], AF.Exp, scale=-1.0)
                kb.act(ax[0:32, 0:NT], ax[0:32, 0:NT], AF.Ln, bias=1.0)
                kb.ts(mx[0:32, 0:NT], x[0:32, 0:NT], 0.0, None, ALU.max)
                kb.tt(dtT[:, 0:NT], mx[0:32, 0:NT], ax[0:32, 0:NT], ALU.add)
                kb.ts(aT[:, 0:NT], dtT[:, 0:NT], negA[:, li:li + 1], None, ALU.mult)
        linear(wview(m_w_in, li), 8, 0, 5152, 512, lambda kc: xn[:, kc, 0:NT], NT, ev_in)
        HT = HTp[:, li, :]
        if sample:
            for si in range(NSB):
                kb.dma('sp', HT, V(st_ssm.ap[li, si], None))
                kb.cp(HTb.v(), HT, eng='act')
                ssd_chunk(li, si * NST, NST, HT)
                kb.dma('sp', V(o_ssms.h[li, si], o_ssms), HT)
        else:
            kb.cp(HTb.v(), HT, eng='act')
            for c in range(NT // 128):
                ssd_chunk(li, c * 128, 128, HT)

        def ev_out(m, ps, w):
            for (c0, ln, mc) in segs:
                kb.stt(h[:, m, c0:c0 + ln], ps[:, c0:c0 + ln], GTM[:, 2 * li, m, mc:mc + 1], h[:, m, c0:c0 + ln], ALU.mult, ALU.add)
        linear(wview(m_w_out, li), 16, 0, D, 256, lambda kc: yn[:, kc, 0:NT], NT, ev_out)
        ffn(li, h, NT, segs, sample)

    pseg = [(0, TILE, 0)]
    sseg = [(i * NST, NST, 1 + i) for i in range(NSB)]
    for ti in range(A_TILES):
        kb.dma('sp', hT.v(), V(xp.ap[:, :, ti * TILE:(ti + 1) * TILE], None))
        for li in range(2):
            a_layer(li, hT, TILE, pseg, False)
        kb.dma('sp', V(h1s.h[PADC // 128 + 4 * ti:PADC // 128 + 4 * ti + 4].rearrange("b p k c -> p k b c"), h1s),
               hT.v().re("p k (b c) -> p k b c", c=128))
    for li in range(2):
        kb.dma('sp', V(o_ssmp.h[li], o_ssmp), HTp[:, li, :])
    kb.dma('sp', o_convp.v(), convc.v())
    zero = FR.next()
    kb.memset(zero.v(), 0.0)
    for bq in range(PADC // 128):
        for hf in range(2):
            kb.dma('sp', V(h1s.h[bq][:, 4 * hf:4 * hf + 4, :], h1s), zero.v().re("p (k c) -> p k c", c=128))
    kb.dma('sp', hS.v(), xs)
    for li in range(2):
        a_layer(li, hS, NS, sseg, True)
    kb.dma('sp', o_convs.v(), convcs.v())

    if not DO_B:
        kb.dma('sp', o_ffnp.v(), ffnc.v()); kb.dma('sp', o_ffns.v(), ffncs.v())
        kb.finish(outs)
        return nc, es

    g = nc.gpsimd
    reg = es.enter_context(g.register("rkreg"))
    g.load(reg, rk.ap[0:1, 0:1])
    off = g.snap(reg)
    mview = HTp.v().re("p l n -> p (l n)").bitcast(BF16)
    am = mview[:, 0:24 * 128].re("p (t q) -> p t q", q=128)
    sm = mview[:, 24 * 128:45 * 128].re("p (t q) -> p t q", q=128)
    smn = mview[0:NST, 45 * 128:48 * 128].re("p (t q) -> p t q", q=128)
    kb.dma('pool', mview[:, 0:24 * 128], amask)
    kb.dma('pool', mview[:, 24 * 128:45 * 128], smask)
    kb.dma('pool', mview[0:NST, 45 * 128:48 * 128], smaskn)
    for v in vsl:
        kb.memset(v[:, :, 64:65], 1.0)
    KTS = Ring([xdt0, xdt1, xdd]); VSL = Ring(vsl)
    PSa = Ring(PSall[4:8])
    wkv_v = wview(w_kv, None)
    ropek_t = [None, None]

    def rope_evac(ps, NT, rc, rs, out_bf, out_f32_dma=None):
        kraw = mhs.next()
        kb.cp(kraw[:, 0:NT], ps[:, 0:NT], eng='act')
        psr = PS.next()
        kb.mm(psr[:, 0:NT], rotb.v(), kraw[:, 0:NT])
        t1 = FR.next(); t2 = FR.next()
        kb.tt(t1[:, 0:NT], ps[:, 0:NT], rc, ALU.mult)
        kb.tt(t2[:, 0:NT], psr[:, 0:NT], rs, ALU.mult)
        kb.tt(t1[:, 0:NT], t1[:, 0:NT], t2[:, 0:NT], ALU.add)
        if out_f32_dma is not None:
            kb.dma('sp', out_f32_dma, t1[:, 0:NT])
        kb.cp(out_bf, t1[:, 0:NT])

    def load_rope(c0, NT, src, col0):
        kb.dma('sp', ropet[:, :, 0:NT], V(src.ap[:, :, col0:col0 + NT], None))
        return ropet[:, 0, 0:NT], ropet[:, 1, 0:NT]

    def kv_tile(b0, nblk):
        NT = nblk * 128
        kb.dma('pool', hT[:, :, 0:NT].re("p k (b c) -> p k b c", c=128),
               V(h1s.h[bass.ds(off + b0, nblk)].rearrange("b p k c -> p k b c"), h1s))
        if KVS <= 1:
            return
        rms_mod(hT, NT, 8, [(0, NT, 0)], None, xn)
        rc, rs = load_rope(0, NT, ropek, b0 * 128)
        own = b0 >= 17
        if KVS <= 2:
            return

        def ev_k(j, ps, w):
            od = V(o_kp.h[:, j, (b0 - 17) * 128:(b0 - 17) * 128 + NT], o_kp) if own else None
            rope_evac(ps, NT, rc, rs, big[:, j, 0:NT], od)
        linear(wkv_v, 8, 0, 3072, 512, lambda kc: xn[:, kc, 0:NT], NT, ev_k)
        if KVS <= 3:
            return
        for blk in range(nblk):
            kb.dma('sp', V(kts.h[:, b0 + blk, :, :], kts), big[:, 0:24, blk * 128:(blk + 1) * 128])
        if KVS <= 4:
            return
        nxt = wload(wkv_v, 8, 3072, 512)
        for fg in range(6):
            wv = nxt
            if fg + 1 < 6:
                nxt = wload(wkv_v, 8, 3072 + (fg + 1) * 512, 512)
            for blk in range(nblk):
                ps = PS.next()
                for kc in range(8):
                    kb.mm(ps[:, 0:512], xn[:, kc, blk * 128:(blk + 1) * 128], wv[:, kc, :], kc == 0, kc == 7)
                vf = FR.next()
                kb.cp(vf.v(), ps.v(), eng='act')
                if own:
                    kb.dma('sp', V(o_vp.h[b0 - 17 + blk][:, fg * 512:(fg + 1) * 512], o_vp), vf.v())
                vb = mhs.next()
                kb.cp(vb.v(), vf.v())
                kb.dma('sp', V(vts.h[b0 + blk][:, fg * 512:(fg + 1) * 512], vts), vb.v())

    def early():
        kb.dma('sp', o_ffnp.v(), ffnc.v()); kb.dma('sp', o_ffns.v(), ffncs.v())
        kb.finish(outs)
        return nc, es
    if B_STAGE == 0:
        return early()
    kv_tile(0, 1)
    if B_STAGE == 1:
        return early()
    for t in range(8):
        kv_tile(1 + 4 * t, 4)
    if B_STAGE == 2:
        return early()

    rms_mod(hS, NS, 8, sseg, None, xn)
    rcs, rss = load_rope(0, NS, ropes, 0)

    def ev_ks(j, ps, w):
        rope_evac(ps, NS, rcs, rss, kns[:, j, :], V(o_ks.h[:, j, :], o_ks))
    linear(wkv_v, 8, 0, 3072, 512, lambda kc: xn[:, kc, 0:NS], NS, ev_ks)
    nxt = wload(wkv_v, 8, 3072, 512)
    for fg in range(6):
        wv = nxt
        if fg + 1 < 6:
            nxt = wload(wkv_v, 8, 3072 + (fg + 1) * 512, 512)
        for si in range(NSB):
            ps = PS.next()
            for kc in range(8):
                kb.mm(ps[0:NST, 0:512], xn[:, kc, si * NST:(si + 1) * NST], wv[:, kc, :], kc == 0, kc == 7)
            vf = FR.next()
            kb.cp(vf[0:NST, :], ps[0:NST, :], eng='act')
            kb.dma('sp', V(o_vs.h[si][:, fg * 512:(fg + 1) * 512], o_vs), vf[0:NST, :])
            vb = mhs.next()
            kb.cp(vb[0:NST, :], vf[0:NST, :])
            kb.dma('sp', V(vns.h[si][:, fg * 512:(fg + 1) * 512], vns), vb[0:NST, :])

    if B_STAGE == 3:
        return early()
    qT = big
    oh = big

    def normalize(psO, ncols, out3, nh, w):
        osb = FR.next()
        kb.cp(osb[0:65, 0:ncols], psO[0:65, 0:ncols], eng='act')
        psl = PSa.next()
        kb.mm(psl[0:64, 0:ncols], l64[0:65, 0:64], osb[0:65, 0:ncols])
        rl = FR.next()
        kb.ts(rl[0:64, 0:ncols], psl[0:64, 0:ncols], 1e-30, None, ALU.add)
        kb.recip(rl[0:64, 0:ncols], rl[0:64, 0:ncols])
        kb.tt(out3, osb[0:64, 0:ncols].re("d (h q) -> d h q", q=w), rl[0:64, 0:ncols].re("d (h q) -> d h q", q=w), ALU.mult)

    def attn_prompt(qb0, nqb):
        for i in range(nqb):
            qb = qb0 + i
            tiles = [(gg, d) for gg in range(3) for d in range(NDEL[gg])]
            for hq in range(4):
                kb.mm(PSall[hq][0:65, 0:512], zerob[:, 0:65], zerob[:, 0:512], True, False)
            for ti, (gg, d) in enumerate(tiles):
                kbk = qb - d
                kt_t = KTS.next(); vt = VSL.next()
                ktv = kt_t[:, 0:1024].re("p (c k) -> p c k", k=128)
                kb.dma('sp', ktv, V(kts.h[:, kbk, gg * 8:(gg + 1) * 8, :], kts))
                kb.dma('sp', vt[:, :, 0:64], V(vts.h[kbk][:, gg * 1024:(gg + 1) * 1024].rearrange("k (h d) -> k h d", d=64), vts))
                mk = am[:, MIDX0[gg] + d, :]
                for hq in range(4):
                    ps = PSa.next()
                    for e in range(4):
                        hs = 4 * hq + e
                        pair, half = hs // 2, hs % 2
                        sl = slice(e * 128, (e + 1) * 128)
                        kb.mm(ps[:, sl], ktv[64 * half:64 * half + 64, pair, :],
                              qT[64 * half:64 * half + 64, gg * 8 + pair, i * 128:(i + 1) * 128], True, False)
                        if kbk <= 16:
                            kb.mm(ps[:, sl], identb.v(), mk, False, False)
                            kb.mm(ps[:, sl], identb.v(), negb.v(), False, True)
                        else:
                            kb.mm(ps[:, sl], identb.v(), mk, False, True)
                    pt = mhs.next()
                    kb.act(pt.v(), ps.v(), AF.Exp, scale=0.125)
                    for e in range(4):
                        hs = 4 * hq + e
                        kb.mm(PSall[hq][0:65, e * 128:(e + 1) * 128], vt[:, hs, 0:65], pt[:, e * 128:(e + 1) * 128],
                              False, False)
            for hq in range(4):
                kb.mm(PSall[hq][0:65, 0:512], zerob[:, 0:65], zerob[:, 0:512], False, True)
            for hq in range(4):
                normalize(PSall[hq], 512, oh[0:64, 24 + 4 * hq:28 + 4 * hq, i * 128:(i + 1) * 128], 4, 128)

    def attn_sample():
        for si in range(NSB):
            tiles = []
            for gg in range(3):
                tiles += [(gg, idx) for idx in range(NCACHE[gg])] + [(gg, -1)]
            kb.mm(PSall[0][0:65, 0:128], zerob[:, 0:65], zerob[:, 0:128], True, False)
            for ti, (gg, idx) in enumerate(tiles):
                vt = VSL.next()
                if idx >= 0:
                    kt_t = KTS.next()
                    ktv = kt_t[:, 0:1024].re("p (c k) -> p c k", k=128)
                    kb.dma('pool', ktv, V(ckT[gg].ap[si, :, :, idx * 128:(idx + 1) * 128].rearrange("c p k -> p c k"), None))
                    kb.dma('pool', vt[:, :, 0:64], V(cvv[gg].ap[si, idx * 128:(idx + 1) * 128, :].rearrange("k (h d) -> k h d", d=64), None))
                    nk = 128
                    mk = sm[:, SM0[gg] + idx, :]
                else:
                    ktv = kns[:, gg * 8:(gg + 1) * 8, si * NST:(si + 1) * NST]
                    kb.dma('sp', vt[0:NST, :, 0:64], V(vns.h[si][:, gg * 1024:(gg + 1) * 1024].rearrange("k (h d) -> k h d", d=64), vns))
                    nk = NST
                    mk = smn[:, gg, :]
                ps = PSa.next()
                for hs in range(16):
                    pair, half = hs // 2, hs % 2
                    sl = slice(hs * NST, (hs + 1) * NST)
                    kb.mm(ps[0:nk, sl], ktv[64 * half:64 * half + 64, pair, 0:nk],
                          qT[64 * half:64 * half + 64, gg * 8 + pair, si * NST:(si + 1) * NST], True, False)
                    kb.mm(ps[0:nk, sl], identb[0:nk, 0:nk], mk[0:nk, sl], False, True)
                pt = mhs.next()
                kb.act(pt[0:nk, 0:128], ps[0:nk, 0:128], AF.Exp, scale=0.125)
                for hs in range(16):
                    sl = slice(hs * NST, (hs + 1) * NST)
                    kb.mm(PSall[0][0:65, sl], vt[0:nk, hs, 0:65], pt[0:nk, sl], False, False)
            kb.mm(PSall[0][0:65, 0:128], zerob[:, 0:65], zerob[:, 0:128], False, True)
            normalize(PSall[0], 128, oh[0:64, 24:40, si * NST:(si + 1) * NST], 16, NST)

    def b_layer(lb, h, NT, segs, sample, qb0):
        li = 2 + lb
        rms_mod(h, NT, 2 * li, segs, None, xn)
        if sample:
            rc, rs = load_rope(0, NT, ropes, 0)
        else:
            rc, rs = load_rope(0, NT, ropek, qb0 * 128)

        def ev_q(j, ps, w):
            rope_evac(ps, NT, rc, rs, qT[:, j, 0:NT])
        linear(wview(w_q, lb), 8, 0, 3072, 512, lambda kc: xn[:, kc, 0:NT], NT, ev_q)
        if sample:
            attn_sample()
        else:
            attn_prompt(qb0, NT // 128)

        def ev_o(m, ps, w):
            for (c0, ln, mc) in segs:
                kb.stt(h[:, m, c0:c0 + ln], ps[:, c0:c0 + ln], GTM[:, 2 * li, m, mc:mc + 1], h[:, m, c0:c0 + ln], ALU.mult, ALU.add)
        linear(wview(w_o, lb, 64), 16, 0, D, 256, lambda kc: oh[0:64, 24 + kc, 0:NT], NT, ev_o, npart=64)
        ffn(li, h, NT, segs, sample)

    yfin = V(yn.h[:].rearrange("p a b -> p (a b)").bitcast(F32), yn)

    def final(h, NT, segs, dst):
        yf = yfin[:, 0:8 * NT].re("p (k t) -> p k t", t=NT)
        rms_mod(h, NT, None, segs, P('final_norm'), yf)
        kb.dma('sp', dst, yf)

    kb.dma('pool', hT[:, :, 0:128].re("p k (b c) -> p k b c", c=128),
           V(h1s.h[bass.ds(off + 16, 1)].rearrange("b p k c -> p k b c"), h1s))
    for lb in range(2):
        b_layer(lb, hT, 128, [(0, 128, 0)], False, 16)
    if B_STAGE == 4:
        return early()
    for li in (2, 3):
        kb.ts(fc4[:, li, :, :], fc4[:, li, :, :], par[:, 64:65], None, ALU.mult)
    for t in range(4):
        kb.dma('pool', hT.v().re("p k (b c) -> p k b c", c=128),
               V(h1s.h[bass.ds(off + 17 + 4 * t, 4)].rearrange("b p k c -> p k b c"), h1s))
        for lb in range(2):
            b_layer(lb, hT, TILE, pseg, False, 17 + 4 * t)
        final(hT, TILE, pseg, V(o_yp.h[:, :, t * TILE:(t + 1) * TILE], o_yp))
    if B_STAGE == 5:
        return early()
    for lb in range(2):
        b_layer(lb, hS, NS, sseg, True, 0)
    final(hS, NS, sseg, o_ys.v())
    kb.dma('sp', o_ffnp.v(), ffnc.v()); kb.dma('sp', o_ffns.v(), ffncs.v())
    kb.finish(outs)
    return nc, es


PK_OFF = {}
PK_N = 0
DO_B = True
B_STAGE = 9
KVS = 9
A_TILES = NTILES


def _pk_layout():
    global PK_N
    o = 0

    def add(key, n):
        nonlocal o
        PK_OFF[key] = (o, n)
        o += n
    for li in range(4):
        add(('ada_b', li), 48); add(('norm_mix', li), 8); add(('norm_ffn', li), 8)
        add(('ffn_conv_w', li), 132); add(('ffn_conv_b', li), 44)
    for li in range(2):
        add(('m_conv_w', li), 96); add(('m_conv_b', li), 24); add(('m_norm', li), 16)
        add(('m_dt_bias', li), 1); add(('A_log', li), 1); add(('m_D', li), 16)
    add('kv_norm', 8); add('kv_ada_b', 16); add('final_norm', 8)
    PK_N = o


_pk_layout()


def fm(v):
    return np.ascontiguousarray(np.asarray(v, np.float32).reshape(-1, 128).T)


def fm2(a):
    a = np.asarray(a, np.float32)
    T_, F_ = a.shape
    return np.ascontiguousarray(a.T.reshape(F_ // 128, 128, T_).transpose(1, 0, 2))


def convw(w):
    K = w.shape[0]
    return np.ascontiguousarray(np.asarray(w, np.float32).T.reshape(-1, 128, K).transpose(1, 0, 2).reshape(128, -1))


def col32(v):
    o = np.zeros((128, 1), np.float32)
    o[:32, 0] = v
    return o


def build_pack(p):
    pk = np.zeros((128, PK_N), np.float32)

    def put(key, arr):
        o, n = PK_OFF[key]
        assert arr.shape == (128, n), (key, arr.shape, n)
        pk[:, o:o + n] = arr
    for li in range(4):
        put(('ada_b', li), fm(p['ada_b'][li])); put(('norm_mix', li), fm(p['norm_mix'][li]))
        put(('norm_ffn', li), fm(p['norm_ffn'][li]))
        put(('ffn_conv_w', li), convw(p['ffn_conv_w'][li])); put(('ffn_conv_b', li), fm(p['ffn_conv_b'][li]))
    for li in range(2):
        put(('m_conv_w', li), convw(p['m_conv_w'][li])); put(('m_conv_b', li), fm(p['m_conv_b'][li]))
        put(('m_norm', li), fm(p['m_norm'][li]))
        put(('m_dt_bias', li), col32(p['m_dt_bias'][li])); put(('A_log', li), col32(p['m_A_log'][li]))
        put(('m_D', li), fm(np.repeat(np.asarray(p['m_D'][li], np.float32), 64)))
    put('kv_norm', fm(p['kv_norm'])); put('kv_ada_b', fm(p['kv_ada_b'])); put('final_norm', fm(p['final_norm']))
    return pk


def rope_tab(pos):
    half = 32
    inv = (np.float32(10000.0) ** (-np.arange(half, dtype=np.float32) * np.float32(2.0 / 64))).astype(np.float32)
    ang = pos.astype(np.float32)[None, :] * inv[:, None]
    c = np.cos(ang).astype(np.float32); s_ = np.sin(ang).astype(np.float32)
    idx = np.arange(128) % 32
    return np.ascontiguousarray(np.stack([c[idx], s_[idx]], axis=1))


def consts():
    c = {}
    c['c_if'] = np.eye(128, dtype=np.float32)
    c['c_ones'] = np.ones((128, 128), np.float32)
    s_ = np.arange(128)
    c['c_tri'] = (s_[:, None] <= s_[None, :]).astype(np.float32)
    sel = np.zeros((32, 32, 128), np.float32)
    for h in range(32):
        sel[h, h, :] = 1.0
    c['c_sel'] = sel.reshape(32, -1)
    ex = np.zeros((32, 16, 128), np.float32)
    for pr in range(16):
        ex[2 * pr, pr, 0:64] = 1.0
        ex[2 * pr + 1, pr, 64:128] = 1.0
    c['c_exp'] = ex.reshape(32, -1)
    rot = np.zeros((128, 128), np.float32)
    for m in range(128):
        if m % 64 < 32:
            rot[m + 32, m] = -1.0
        else:
            rot[m - 32, m] = 1.0
    c['c_rot'] = rot
    l64 = np.zeros((128, 64), np.float32); l64[64, :] = 1.0
    c['c_l64'] = l64
    am = np.zeros((128, 24, 128), np.float32)
    k_ = np.arange(128)[:, None]; q_ = np.arange(128)[None, :]
    for gg, (win, dil) in enumerate(DIL):
        for d in range(NDEL[gg]):
            dq = 128 * d + q_ - k_
            ok = (dq >= 0) & (dq <= win) & (dq % dil == 0)
            am[:, MIDX0[gg] + d, :] = np.where(ok, 0.0, NEG)
    c['amask'] = am.reshape(128, -1)
    sm = np.zeros((128, 21, 16, NST), np.float32)
    smn = np.zeros((NST, 3, 16, NST), np.float32)
    t_ = np.arange(NST)[None, :]
    for gg, (win, dil) in enumerate(DIL):
        for idx in range(NCACHE[gg]):
            row = (2048 - win) + idx * 128 + np.arange(128)[:, None]
            dist = 2048 + t_ - row
            ok = (dist >= 0) & (dist <= win) & (dist % dil == 0)
            sm[:, SM0[gg] + idx, :, :] = np.where(ok, 0.0, NEG)[:, None, :]
        dist = t_ - np.arange(NST)[:, None]
        ok = (dist >= 0) & (dist % dil == 0)
        smn[:, gg, :, :] = np.where(ok, 0.0, NEG)[:, None, :]
    c['smask'] = sm.reshape(128, -1)
    c['smaskn'] = smn.reshape(NST, -1)
    c['ropes'] = rope_tab(8192 + (np.arange(NS) % NST))
    return c


def make_in_maps(p, cores=range(8)):
    cst = consts()
    pk = build_pack(p)
    shared = dict(pk=pk, ada_w=p['ada_w'], kv_ada_w=p['kv_ada_w'], m_w_in=p['m_w_in'], m_w_out=p['m_w_out'],
                  w_kv=p['w_kv'], w_q=p['w_q'], w_o=p['w_o'], ffn_w_up=p['ffn_w_up'], ffn_w_down=p['ffn_w_down'])
    for k in ('c_if', 'c_ones', 'c_tri', 'c_sel', 'c_exp', 'c_rot', 'c_l64', 'amask', 'smask', 'smaskn', 'ropes'):
        shared[k] = cst[k]
    in_maps = []
    for c in cores:
        b, r = c // 4, c % 4
        sbs = slice(4 * c, 4 * c + 4)
        m = dict(shared)
        m['xp'] = fm2(p['x_prompt'][b])
        m['xs'] = fm2(p['x_sample'][sbs].reshape(NS, D))
        m['cs'] = fm2(np.concatenate([p['c_prompt'][b:b + 1], p['c_sample'][sbs]], axis=0))
        par = np.zeros((128, 72), np.float32)
        par[:, 0:32] = (np.arange(32) % 2 == 0); par[:, 32:64] = (np.arange(32) % 2 == 1)
        par[:, 64] = 1.0 if r > 0 else 0.0
        m['c_par'] = par
        m['c_neg'] = np.full((128, 128), 0.0 if r > 0 else NEG, np.float32)
        m['rk'] = np.array([[16 * r, 0, 0, 0]], np.int32)
        m['ropek'] = rope_tab(2048 * r - 2176 + np.arange(NB_LOC * 128))
        m['st_ssm'] = np.ascontiguousarray(p['state_ssm'][:, sbs].transpose(0, 1, 4, 2, 3).reshape(2, NSB, 128, 2048))
        sc = p['state_conv'][:, sbs]
        m['st_conv'] = np.ascontiguousarray(sc.reshape(2, NSB, 3, 24, 128).transpose(4, 0, 1, 3, 2).reshape(128, -1))
        sf = p['state_ffn_conv'][:, sbs]
        m['st_ffn'] = np.ascontiguousarray(sf.reshape(4, NSB, 2, 44, 128).transpose(4, 0, 1, 3, 2).reshape(128, -1))
        for gg, (win, dil) in enumerate(DIL):
            ck = p['cache_k'][sbs, 2048 - win:, 16 * gg:16 * gg + 16, :]
            m['ck%d' % gg] = np.ascontiguousarray(ck.reshape(NSB, win, 8, 128).transpose(0, 2, 3, 1))
            cv = p['cache_v'][sbs, 2048 - win:, 16 * gg:16 * gg + 16, :]
            m['cv%d' % gg] = np.ascontiguousarray(cv.reshape(NSB, win, 1024))
        in_maps.append(m)
    return in_maps


def kernel(**inp):
    p = {k: np.asarray(v) for k, v in inp.items()}
    nc, es = build_program()
    in_maps = make_in_maps(p)
    res = run_bass_kernel_spmd(nc, in_maps, core_ids=list(range(8)))
    es.close()
    R = res.results
    return assemble(R)


def unfm2(a):
    return np.ascontiguousarray(a.transpose(2, 1, 0).reshape(a.shape[2], -1))


def assemble(R):
    f32 = np.float32
    y_p = np.zeros((2, SEQ, D), f32); y_s = np.zeros((32, NST, D), f32)
    ssm_p = np.zeros((2, 2, 32, 64, 128), f32); ssm_s = np.zeros((2, 32, 32, 64, 128), f32)
    conv_p = np.zeros((2, 2, 3, 3072), f32); conv_s = np.zeros((2, 32, 3, 3072), f32)
    ffn_p = np.zeros((4, 2, 2, 5632), f32); ffn_s = np.zeros((4, 32, 2, 5632), f32)
    k_p = np.zeros((2, 2048, 48, 64), f32); k_s = np.zeros((32, NST, 48, 64), f32)
    v_p = np.zeros((2, 2048, 48, 64), f32); v_s = np.zeros((32, NST, 48, 64), f32)
    for c in range(8):
        b, r = c // 4, c % 4
        o = R[c]
        sbs = slice(4 * c, 4 * c + 4)
        y_p[b, SPAN * r:SPAN * (r + 1)] = unfm2(o['o_yp'])
        y_s[sbs] = unfm2(o['o_ys']).reshape(NSB, NST, D)
        ssm_s[:, sbs] = o['o_ssms'].reshape(2, NSB, 128, 32, 64).transpose(0, 1, 3, 4, 2)
        conv_s[:, sbs] = o['o_convs'].reshape(128, 2, NSB, 24, 3).transpose(1, 2, 4, 3, 0).reshape(2, NSB, 3, 3072)
        ffn_s[:, sbs] = o['o_ffns'].reshape(128, 4, NSB, 44, 2).transpose(1, 2, 4, 3, 0).reshape(4, NSB, 2, 5632)
        k_s[sbs] = unfm2(o['o_ks']).reshape(NSB, NST, 48, 64)
        v_s[sbs] = o['o_vs'].reshape(NSB, NST, 48, 64)
        if r == 3:
            ssm_p[:, b] = o['o_ssmp'].reshape(2, 128, 32, 64).transpose(0, 2, 3, 1)
            conv_p[:, b] = o['o_convp'].reshape(128, 2, 24, 3).transpose(1, 3, 2, 0).reshape(2, 3, 3072)
            ffn_p[:, b] = o['o_ffnp'].reshape(128, 4, 44, 2).transpose(1, 3, 2, 0).reshape(4, 2, 5632)
            k_p[b] = unfm2(o['o_kp']).reshape(2048, 48, 64)
            v_p[b] = o['o_vp'].reshape(2048, 48, 64)
    return (y_p, y_s, ssm_p, ssm_s, conv_p, conv_s, ffn_p, ffn_s, k_p, k_s, v_p, v_s)
```
